# Optimizing a Trainium2 kernel written in Bass

```python
import math
import jax, jax.numpy as jnp
from jax import lax
import numpy as np

D_MODEL = 4096
BATCH = 4
SEQ = 2048
DEPTH = 1

D_FF = 11008
ATTN_HEAD_DIM = 128
N_ATTN_HEADS = D_MODEL // (2 * ATTN_HEAD_DIM)
D_ATTN = N_ATTN_HEADS * ATTN_HEAD_DIM
DILATED_CONFIGS = ((128, 1), (512, 4), (2048, 16))
ATTN_BLOCK = 128
DN_HEAD_DIM = 128
N_DN_HEADS = D_MODEL // (2 * DN_HEAD_DIM)
D_DN = N_DN_HEADS * DN_HEAD_DIM
CONV_WIDTH = 4
CHUNK = 64
D_MIX = D_ATTN + D_DN
IN_SPLITS = (D_ATTN, D_ATTN, D_ATTN, 3 * D_DN, D_DN, N_DN_HEADS, N_DN_HEADS)
D_IN_PROJ = sum(IN_SPLITS)
EPS = 1e-6

kernel_name = "hymba_dilated_swa_gated_deltanet_macaron"


def _rmsnorm(x, w):
    xf = x.astype(jnp.float32)
    y = xf * lax.rsqrt(jnp.mean(xf * xf, axis=-1, keepdims=True) + EPS)
    return (y * w.astype(jnp.float32)).astype(x.dtype)


def _swiglu(h, w_gate, w_up, w_down):
    return (jax.nn.silu(h @ w_gate) * (h @ w_up)) @ w_down


def _band_attention(q, k, v, steps):
    G, L, H, Dh = q.shape
    nb = -(-L // ATTN_BLOCK)
    lp = nb * ATTN_BLOCK
    qb = jnp.pad(q, ((0, 0), (0, lp - L), (0, 0), (0, 0))).reshape(G, nb, ATTN_BLOCK, H, Dh)

    def band(t):
        t = jnp.pad(t, ((0, 0), (ATTN_BLOCK, lp - L), (0, 0), (0, 0)))
        t = t.reshape(G, nb + 1, ATTN_BLOCK, H, Dh)
        return jnp.concatenate([t[:, :-1], t[:, 1:]], axis=2)

    kw, vw = band(k), band(v)
    s = jnp.einsum('gnqhd,gnkhd->gnhqk', qb, kw, preferred_element_type=jnp.float32) * (Dh ** -0.5)
    i = jnp.arange(ATTN_BLOCK)[:, None]
    j = jnp.arange(2 * ATTN_BLOCK)[None, :]
    dist = i + ATTN_BLOCK - j
    kpos = jnp.arange(nb)[:, None, None] * ATTN_BLOCK - ATTN_BLOCK + j
    valid = (dist >= 0) & (dist <= steps) & (kpos >= 0)
    s = jnp.where(valid[:, None], s, -jnp.inf)
    m = jnp.max(s, axis=-1, keepdims=True)
    p = jnp.exp(s - m)
    den = jnp.sum(p, axis=-1)
    num = jnp.einsum('gnhqk,gnkhd->gnqhd', p, vw.astype(jnp.float32))
    m = m[..., 0].transpose(0, 1, 3, 2).reshape(G, lp, H)[:, :L]
    den = den.transpose(0, 1, 3, 2).reshape(G, lp, H)[:, :L]
    num = num.reshape(G, lp, H, Dh)[:, :L]
    return m, num, den


def _dilated_attention(q, k, v):
    B, S, H, Dh = q.shape
    ms, nums, dens = [], [], []
    for window, d in DILATED_CONFIGS:
        L = S // d

        def to_res(t):
            return t.reshape(B, L, d, H, Dh).transpose(0, 2, 1, 3, 4).reshape(B * d, L, H, Dh)

        def from_res(t):
            rest = t.shape[2:]
            return jnp.swapaxes(t.reshape(B, d, L, *rest), 1, 2).reshape(B, S, *rest)

        m, num, den = _band_attention(to_res(q), to_res(k), to_res(v), window // d)
        ms.append(from_res(m)); nums.append(from_res(num)); dens.append(from_res(den))
    m_all = jnp.stack(ms)
    w = jnp.exp(m_all - jnp.max(m_all, axis=0, keepdims=True))
    num = jnp.sum(w[..., None] * jnp.stack(nums), axis=0)
    den = jnp.sum(w * jnp.stack(dens), axis=0)
    return num / den[..., None]


def _causal_conv(x, w):
    S = x.shape[1]
    xp = jnp.pad(x, ((0, 0), (CONV_WIDTH - 1, 0), (0, 0)))
    return sum(w[i] * xp[:, i:i + S] for i in range(CONV_WIDTH))


def _gated_delta_rule(q, k, v, g, beta):
    B, S, H, Dk = q.shape
    Dv = v.shape[-1]
    N = S // CHUNK

    def chunks(t):
        return jnp.swapaxes(t.reshape(B, N, CHUNK, H, *t.shape[3:]), 2, 3)

    q, k, v, g, beta = (chunks(t.astype(jnp.float32)) for t in (q, k, v, g, beta))
    gc = jnp.cumsum(g, axis=-1)
    idx = jnp.arange(CHUNK)
    incl = idx[:, None] >= idx[None, :]
    strict = idx[:, None] > idx[None, :]
    decay = jnp.exp(jnp.where(incl, gc[..., :, None] - gc[..., None, :], -jnp.inf))
    kb = k * beta[..., None]
    kk = jnp.einsum('bnhid,bnhjd->bnhij', kb, k)
    a = jnp.where(strict, kk * decay, 0.0) + jnp.eye(CHUNK, dtype=jnp.float32)
    rhs = jnp.concatenate([kb * jnp.exp(gc)[..., None], v * beta[..., None]], axis=-1)
    sol = lax.linalg.triangular_solve(a, rhs, left_side=True, lower=True, unit_diagonal=True)
    w_c, u_c = sol[..., :Dk], sol[..., Dk:]
    qk = jnp.einsum('bnhid,bnhjd->bnhij', q, k) * decay
    q_dec = q * jnp.exp(gc)[..., None]
    k_dec = k * jnp.exp(gc[..., -1:] - gc)[..., None]
    g_last = jnp.exp(gc[..., -1])

    def step(state, xs):
        wc, uc, qkc, qdc, kdc, glc = xs
        v_new = uc - jnp.einsum('bhcd,bhdv->bhcv', wc, state)
        o = jnp.einsum('bhcd,bhdv->bhcv', qdc, state) + jnp.einsum('bhij,bhjv->bhiv', qkc, v_new)
        state = state * glc[..., None, None] + jnp.einsum('bhcd,bhcv->bhdv', kdc, v_new)
        return state, o

    xs = tuple(jnp.moveaxis(t, 1, 0) for t in (w_c, u_c, qk, q_dec, k_dec, g_last))
    state0 = jnp.zeros((B, H, Dk, Dv), jnp.float32)
    _, o = lax.scan(step, state0, xs)
    return o.transpose(1, 0, 3, 2, 4).reshape(B, S, H, Dv)


def _hybrid_mixer(h, w_in, conv_w, a_log, dt_bias, dn_norm, w_out):
    B, S, _ = h.shape
    proj = h @ w_in
    cuts = [int(c) for c in np.cumsum(IN_SPLITS)[:-1]]
    aq, ak, av, dqkv, dz, db, da = jnp.split(proj, cuts, axis=-1)
    heads_a = lambda t: t.reshape(B, S, N_ATTN_HEADS, ATTN_HEAD_DIM)
    attn = _dilated_attention(heads_a(aq), heads_a(ak), heads_a(av))
    attn = attn.reshape(B, S, D_ATTN).astype(h.dtype)
    dqkv = jax.nn.silu(_causal_conv(dqkv, conv_w))
    dq, dk, dv = jnp.split(dqkv, 3, axis=-1)
    heads_b = lambda t: t.reshape(B, S, N_DN_HEADS, DN_HEAD_DIM).astype(jnp.float32)
    dq, dk, dv = heads_b(dq), heads_b(dk), heads_b(dv)
    l2 = lambda t: t * lax.rsqrt(jnp.sum(t * t, axis=-1, keepdims=True) + EPS)
    dq = l2(dq) * (DN_HEAD_DIM ** -0.5)
    dk = l2(dk)
    beta = jax.nn.sigmoid(db.astype(jnp.float32))
    g = -jnp.exp(a_log.astype(jnp.float32)) * jax.nn.softplus(da.astype(jnp.float32) + dt_bias.astype(jnp.float32))
    o = _gated_delta_rule(dq, dk, dv, g, beta)
    o = o * lax.rsqrt(jnp.mean(o * o, axis=-1, keepdims=True) + EPS) * dn_norm.astype(jnp.float32)
    o = o * jax.nn.silu(heads_b(dz))
    dn = o.reshape(B, S, D_DN).astype(h.dtype)
    return jnp.concatenate([attn, dn], axis=-1) @ w_out


def setup_inputs(seed: int = 0) -> dict:
    key = jax.random.key(seed)
    ks = jax.random.split(key, 20)
    f32 = jnp.float32
    nrm = lambda k, shape, fan_in: jax.random.normal(k, shape, f32) * fan_in ** -0.5
    gain = lambda k, n: 1.0 + 0.05 * jax.random.normal(k, (n,), f32)
    dt = jnp.exp(jax.random.uniform(ks[9], (N_DN_HEADS,), f32, math.log(1e-3), math.log(1e-1)))
    return {
        "x": jax.random.normal(ks[0], (BATCH, SEQ, D_MODEL), f32),
        "ffn1_norm": gain(ks[1], D_MODEL),
        "ffn1_w_gate": nrm(ks[2], (D_MODEL, D_FF), D_MODEL),
        "ffn1_w_up": nrm(ks[3], (D_MODEL, D_FF), D_MODEL),
        "ffn1_w_down": nrm(ks[4], (D_FF, D_MODEL), D_FF),
        "mix_norm": gain(ks[5], D_MODEL),
        "w_in": nrm(ks[6], (D_MODEL, D_IN_PROJ), D_MODEL),
        "conv_w": nrm(ks[7], (CONV_WIDTH, 3 * D_DN), CONV_WIDTH),
        "a_log": jnp.log(jax.random.uniform(ks[8], (N_DN_HEADS,), f32, 1.0, 16.0)),
        "dt_bias": dt + jnp.log(-jnp.expm1(-dt)),
        "dn_norm": gain(ks[10], DN_HEAD_DIM),
        "w_out": nrm(ks[11], (D_MIX, D_MODEL), D_MIX),
        "ffn2_norm": gain(ks[12], D_MODEL),
        "ffn2_w_gate": nrm(ks[13], (D_MODEL, D_FF), D_MODEL),
        "ffn2_w_up": nrm(ks[14], (D_MODEL, D_FF), D_MODEL),
        "ffn2_w_down": nrm(ks[15], (D_FF, D_MODEL), D_FF),
        "final_norm": gain(ks[16], D_MODEL),
    }


def reference(x, ffn1_norm, ffn1_w_gate, ffn1_w_up, ffn1_w_down, mix_norm, w_in, conv_w,
              a_log, dt_bias, dn_norm, w_out, ffn2_norm, ffn2_w_gate, ffn2_w_up, ffn2_w_down,
              final_norm):
    h = x
    for _ in range(DEPTH):
        h = h + 0.5 * _swiglu(_rmsnorm(h, ffn1_norm), ffn1_w_gate, ffn1_w_up, ffn1_w_down)
        h = h + _hybrid_mixer(_rmsnorm(h, mix_norm), w_in, conv_w, a_log, dt_bias, dn_norm, w_out)
        h = h + 0.5 * _swiglu(_rmsnorm(h, ffn2_norm), ffn2_w_gate, ffn2_w_up, ffn2_w_down)
    return _rmsnorm(h, final_norm)
```

```python
import numpy as np
import ml_dtypes
import concourse.bass as bass
import concourse.mybir as mybir
from concourse.bass_utils import run_bass_kernel_spmd

F32 = mybir.dt.float32
BF16 = mybir.dt.bfloat16
ALU = mybir.AluOpType
AF = mybir.ActivationFunctionType
NCORE = 8
EPS = 1e-6


class Op:
    __slots__ = ("eng", "fn", "deps", "kind", "slot", "val", "sig", "ord", "idx")

    def __init__(self, eng, fn, kind):
        self.eng = eng; self.fn = fn; self.kind = kind
        self.deps = []; self.slot = None; self.val = 0; self.sig = False; self.ord = 0


class Prog:
    ENGS = ("pe", "act", "dve", "pool", "sp")

    def __init__(self, nc, stack):
        self.nc = nc
        self.stack = stack
        self.pending = {e: [] for e in self.ENGS}
        self.last_w = {}
        self.readers = {}
        self.slot_cnt = {}
        self.slot_sem = {}
        self.eng_sem = {e: stack.enter_context(nc.semaphore("es_" + e)) for e in ("pe", "act", "dve", "pool")}
        self.eng_cnt = {e: 0 for e in self.ENGS}
        self.waited = {e: {} for e in self.ENGS}
        self.last_op = {e: None for e in self.ENGS}
        self.open_dmas = []

    def _slot(self, slot):
        if slot not in self.slot_sem:
            self.slot_sem[slot] = self.stack.enter_context(self.nc.semaphore("ds_" + slot))
            self.slot_cnt[slot] = 0
        return self.slot_sem[slot]

    def add(self, eng, fn, r=(), w=(), slot=None, cc=False, extra=()):
        kind = "cc" if cc else ("dma" if slot is not None else "cmp")
        op = Op(eng, fn, kind)
        deps = []
        for k in r:
            lw = self.last_w.get(k)
            if lw is not None:
                deps.append(lw)
            if isinstance(k, tuple) and k[0] == "ps":
                deps.extend(rd for rd in self.readers.get(k, ()) if rd.eng != eng)
        for k in w:
            lw = self.last_w.get(k)
            if lw is not None:
                deps.append(lw)
            deps.extend(self.readers.get(k, ()))
        deps.extend(extra)
        for k in r:
            self.readers.setdefault(k, []).append(op)
        for k in w:
            self.last_w[k] = op
            self.readers[k] = []
        seen = set()
        for d in deps:
            if d is op or id(d) in seen:
                continue
            seen.add(id(d))
            if d.kind == "cmp":
                if d.eng == "pe" and eng == "pe":
                    continue
                d.sig = True
            op.deps.append(d)
        if slot is not None:
            self._slot(slot)
            self.slot_cnt[slot] += 1
            op.slot = slot
            op.val = self.slot_cnt[slot] * (1 if cc else 16)
            self.open_dmas.append(op)
        self.pending[eng].append(op)
        self.last_op[eng] = op
        return op

    def barrier(self):
        lasts = [o for o in self.last_op.values() if o is not None and o.kind == "cmp"]
        dmas = list({o.slot: o for o in self.open_dmas}.values())
        self.open_dmas = []
        for e in self.ENGS:
            self.add(e, None, extra=lasts + dmas)

    def emit(self):
        nc = self.nc
        engs = {"pe": "tensor", "act": "scalar", "dve": "vector", "pool": "gpsimd", "sp": "sync"}
        pend = self.pending
        self.pending = {e: [] for e in self.ENGS}
        for e in self.ENGS:
            for op in pend[e]:
                if op.kind == "cmp" and op.sig:
                    self.eng_cnt[e] += 1
                    op.ord = self.eng_cnt[e]
        with nc.Block() as block:
            for e in self.ENGS:
                ops = pend[e]
                if not ops:
                    continue

                def section(engine, ops=ops, e=e):
                    waited = self.waited[e]
                    for op in ops:
                        need = {}
                        for d in op.deps:
                            if d.kind == "cmp":
                                key = ("e", d.eng); v = d.ord
                            else:
                                key = ("s", d.slot); v = d.val
                            if v > need.get(key, 0):
                                need[key] = v
                        for key, v in need.items():
                            if waited.get(key, 0) >= v:
                                continue
                            sem = self.eng_sem[key[1]] if key[0] == "e" else self.slot_sem[key[1]]
                            engine.wait_ge(sem, v)
                            waited[key] = v
                        if op.fn is None:
                            continue
                        ins = op.fn(engine)
                        if op.kind == "cmp":
                            if op.sig:
                                ins.then_inc(self.eng_sem[e], 1)
                        elif op.kind == "dma":
                            ins.then_inc(self.slot_sem[op.slot], 16)
                        else:
                            ins.then_inc(self.slot_sem[op.slot], 1)

                getattr(block, engs[e])(section)


class Cfg:
    def __init__(self, D=4096, FF=11008, S=2048, B=4, NHA=16, NHD=16):
        self.D = D; self.FF = FF; self.S = S; self.B = B; self.NHA = NHA; self.NHD = NHD
        self.KC = D // 128
        self.NJ = FF // 128
        self.DA = NHA * 128; self.DD = NHD * 128
        self.KM = NHA + NHD
        self.WIN = 3 * self.DA + 4 * self.DD + 2 * NHD
        self.TT = 512
        self.TF = min(1024, S)
        self.NGRP = 4
        self.NTT = S // self.TT
        self.NTILE = S // 128
        self.NT = 64
        self.NOFF = self.NTILE + 3
        self.mixstop = 9


def build(cfg, phases=("ffn1", "mix", "ffn2")):
    from contextlib import ExitStack
    D, KC, FF, NJ, S, TT, NTT, NT, NTILE = cfg.D, cfg.KC, cfg.FF, cfg.NJ, cfg.S, cfg.TT, cfg.NTT, cfg.NT, cfg.NTILE
    NHA, NHD, DA, DD, KM, WIN = cfg.NHA, cfg.NHD, cfg.DA, cfg.DD, cfg.KM, cfg.WIN
    nc = bass.Bass("TRN2", target_bir_lowering=False)
    dt_in = lambda name, shape: nc.dram_tensor(name, list(shape), F32, kind="ExternalInput").ap()
    xT = dt_in("xT", [128, KC, S])
    norms = dt_in("norms", [128, 4, KC])
    wg = [dt_in("wg1", [D, FF]), dt_in("wg2", [D, FF])]
    wu = [dt_in("wu1", [D, FF]), dt_in("wu2", [D, FF])]
    wd = [dt_in("wd1", [FF, D]), dt_in("wd2", [FF, D])]
    win = dt_in("win", [D, WIN])
    convw_d = dt_in("convw", [128, 3 * NHD, 4])
    hp_d = dt_in("hp", [128, 2, NHD])
    dnw_d = dt_in("dnw", [128, 128])
    wo = dt_in("wo", [KM * 128, D])
    cst_d = dt_in("cst", [128, 4, 128])
    amask_d = dt_in("amask", [128, cfg.NOFF * 128])
    out = nc.dram_tensor("out", [128, KC, S], F32, kind="ExternalOutput").ap()
    scr_hn = nc.dram_tensor("scr_hn", [128, KC, S], BF16, kind="ExternalOutput").ap()
    scr_mix = nc.dram_tensor("scr_mix", [128, KM, S], BF16, kind="ExternalOutput").ap()

    with ExitStack() as stack:
        P = Prog(nc, stack)
        sb = lambda name, shape, dt=F32: stack.enter_context(nc.sbuf_tensor(name, list(shape), dt))
        ones_f = sb("ones_f", [128, 128])
        normw = sb("normw", [128, 4, KC])
        psum = [stack.enter_context(nc.psum_tensor(f"ps{i}", [128, 512], F32)) for i in range(8)]
        P.add("pool", lambda e: e.memset(ones_f[:], 1.0), w=["ones_f"])
        P.add("sp", lambda e: e.dma_start(out=normw[:], in_=norms), w=["normw"], slot="const")
        NB = {}

        def alloc_norm(stk, tag, nbuf):
            a = lambda name, shape: stk.enter_context(nc.sbuf_tensor(f"{tag}_{name}", list(shape), F32))
            NB.clear()
            NB.update(n=nbuf, cnt=0, banks=(6, 7) if nbuf > 1 else (6,),
                      hc=[a(f"hc{q}", [128, KC, NT]) for q in range(nbuf)], sq=[a(f"sq{q}", [128, KC, NT]) for q in range(nbuf)],
                      rstd=[a(f"rstd{q}", [128, NT]) for q in range(nbuf)], ssum=[a(f"ssum{q}", [128, NT]) for q in range(nbuf)])

        def okeys(t0, n):
            return [("out", q) for q in range(t0 // NT, (t0 + n + NT - 1) // NT)]

        def norm_tokens(t0, from_x, widx, dst_fn, dst_key, bf):
            srcr = xT if from_x else out
            i = NB["cnt"] % NB["n"]; NB["cnt"] += 1
            hc, sq, rstd, ssum = NB["hc"][i], NB["sq"][i], NB["rstd"][i], NB["ssum"][i]
            pb = NB["banks"][i % len(NB["banks"])]
            hk, sk, rk, mk = ("hc", i), ("sq", i), ("rstd", i), ("ssum", i)
            P.add("sp", lambda e: e.dma_start(out=hc[:], in_=srcr[:, :, t0:t0 + NT]),
                  r=[] if from_x else okeys(t0, NT), w=[hk], slot=f"hc{i}")
            P.add("act", lambda e: e.activation(out=sq[:], in_=hc[:], func=AF.Square), r=[hk], w=[sk])
            P.add("dve", lambda e: e.reduce_sum(out=ssum[:], in_=sq[:].rearrange("p c t -> p t c"), axis=mybir.AxisListType.X),
                  r=[sk], w=[mk])
            P.add("pe", lambda e: e.matmul(psum[pb][:, 0:NT], lhsT=ones_f[:], rhs=ssum[:], start=True, stop=True),
                  r=[mk, "ones_f"], w=[("ps", pb)])
            P.add("act", lambda e: e.activation(out=rstd[:], in_=psum[pb][:, 0:NT], func=AF.Sqrt, bias=EPS, scale=1.0 / D),
                  r=[("ps", pb)], w=[rk])
            P.add("dve", lambda e: e.reciprocal(out=rstd[:], in_=rstd[:]), r=[rk], w=[rk])
            P.add("dve", lambda e: e.tensor_tensor(out=sq[:], in0=hc[:],
                                                   in1=rstd[:].unsqueeze(1).broadcast_to([128, KC, NT]), op=ALU.mult),
                  r=[hk, rk], w=[sk])
            P.add("dve", lambda e: e.tensor_tensor(out=dst_fn(), in0=sq[:],
                                                   in1=normw[:, widx, :].unsqueeze(2).broadcast_to([128, KC, NT]),
                                                   op=ALU.mult), r=[sk, "normw"], w=[dst_key])

        def ffn_phase(f, widx, from_x):
            with ExitStack() as fs:
                fsb = lambda name, shape, dt=F32: fs.enter_context(nc.sbuf_tensor(f"f{f}_" + name, list(shape), dt))
                TF = cfg.TF; NH = TF // 512; NTF = S // TF
                alloc_norm(fs, f"f{f}n", 2)
                groups = [list(g) for g in np.array_split(np.arange(NJ), cfg.NGRP) if len(g)]
                GM = max(len(g) for g in groups)
                hn = fsb("hn", [128, KC, TF], BF16)
                wgt = [fsb(f"wgt{q}", [128, KC, 128], BF16) for q in range(2)]
                wut = [fsb(f"wut{q}", [128, KC, 128], BF16) for q in range(2)]
                hid = fsb("hid", [128, GM, TF], BF16)
                wdt = [fsb(f"wdt{q}", [128, GM, 128], BF16) for q in range(2)]
                sil = [fsb(f"sil{q}", [128, 512]) for q in range(2)]
                oldt = [fsb(f"old{q}", [128, 512]) for q in range(2)]
                newt = [fsb(f"new{q}", [128, 512]) for q in range(2)]
                wcnt = [0]; dcnt = [0]; pcnt = [0]; scnt = [0]; ocnt = [0]; dps = [0]

                def load_w(j):
                    q = wcnt[0] % 2; wcnt[0] += 1
                    srcg = wg[f][:, j * 128:(j + 1) * 128].rearrange("(k p) n -> p k n", p=128)
                    srcu = wu[f][:, j * 128:(j + 1) * 128].rearrange("(k p) n -> p k n", p=128)
                    P.add("pool", lambda e: e.dma_start(out=wgt[q][:], in_=srcg), w=[("wgt", q)], slot=f"wg{q}")
                    P.add("pool", lambda e: e.dma_start(out=wut[q][:], in_=srcu), w=[("wut", q)], slot=f"wu{q}")
                    return q

                def load_wd(c, grp):
                    q = dcnt[0] % 2; dcnt[0] += 1
                    src = wd[f][grp[0] * 128:(grp[-1] + 1) * 128, c * 128:(c + 1) * 128].rearrange("(j p) n -> p j n", p=128)
                    P.add("pool", lambda e: e.dma_start(out=wdt[q][:, 0:len(grp), :], in_=src), w=[("wdt", q)], slot=f"wd{q}")
                    return q

                seq = [(t, gi) for t in range(NTF) for gi in range(len(groups))]
                pre = load_w(groups[0][0])
                for si, (t, gi) in enumerate(seq):
                    grp = groups[gi]
                    if gi == 0:
                        for s_ in range(TF // NT):
                            norm_tokens(t * TF + s_ * NT, from_x, widx,
                                        (lambda s_=s_: hn[:, :, s_ * NT:(s_ + 1) * NT]), "hn", True)
                    for jj, j in enumerate(grp):
                        q = pre
                        if jj + 1 < len(grp):
                            pre = load_w(grp[jj + 1])
                        elif si + 1 < len(seq):
                            pre = load_w(groups[seq[si + 1][1]][0])
                        for hf in range(NH):
                            b0 = 2 * (pcnt[0] % 2); pcnt[0] += 1
                            pg, pu = psum[b0], psum[b0 + 1]
                            for (wt, pp, nm, bo) in ((wgt, pg, "wgt", 0), (wut, pu, "wut", 1)):
                                for k in range(KC):
                                    P.add("pe", lambda e, wt=wt, pp=pp, k=k, q=q, hf=hf: e.matmul(
                                        pp[:, :], lhsT=wt[q][:, k, :], rhs=hn[:, k, hf * 512:(hf + 1) * 512],
                                        start=(k == 0), stop=(k == KC - 1)),
                                        r=[(nm, q), "hn"], w=[("ps", b0 + bo)])
                            sq_ = scnt[0] % 2; scnt[0] += 1
                            P.add("act", lambda e, pg=pg, sq_=sq_: e.activation(out=sil[sq_][:], in_=pg[:, :], func=AF.Silu),
                                  r=[("ps", b0)], w=[("sil", sq_)])
                            P.add("dve", lambda e, pu=pu, sq_=sq_, jj=jj, hf=hf: e.tensor_tensor(
                                out=hid[:, jj, hf * 512:(hf + 1) * 512], in0=pu[:, :], in1=sil[sq_][:], op=ALU.mult),
                                r=[("ps", b0 + 1), ("sil", sq_)], w=[("hid", jj, hf)])
                    src_old = xT if (from_x and gi == 0) else out
                    dpre = load_wd(0, grp)
                    for c in range(KC):
                        q = dpre
                        if c + 1 < KC:
                            dpre = load_wd(c + 1, grp)
                        for hf in range(NH):
                            pb = 4 + (dps[0] % 2); dps[0] += 1
                            oq = ocnt[0] % 2; ocnt[0] += 1
                            tk0 = t * TF + hf * 512
                            P.add("sp", lambda e, c=c, tk0=tk0, oq=oq, src_old=src_old: e.dma_start(out=oldt[oq][:], in_=src_old[:, c, tk0:tk0 + 512]),
                                  r=[] if src_old is xT else okeys(tk0, 512), w=[("old", oq)], slot=f"old{oq}")
                            for jj in range(len(grp)):
                                P.add("pe", lambda e, pb=pb, q=q, jj=jj, hf=hf, L=len(grp): e.matmul(
                                    psum[pb][:, :], lhsT=wdt[q][:, jj, :], rhs=hid[:, jj, hf * 512:(hf + 1) * 512],
                                    start=(jj == 0), stop=(jj == L - 1)),
                                    r=[("wdt", q), ("hid", jj, hf)], w=[("ps", pb)])
                            P.add("dve", lambda e, pb=pb, oq=oq: e.scalar_tensor_tensor(
                                out=newt[oq][:], in0=psum[pb][:, :], scalar=0.5, in1=oldt[oq][:], op0=ALU.mult, op1=ALU.add),
                                r=[("ps", pb), ("old", oq)], w=[("new", oq)])
                            P.add("sp", lambda e, c=c, tk0=tk0, oq=oq: e.dma_start(out=out[:, c, tk0:tk0 + 512], in_=newt[oq][:]),
                                  r=[("new", oq)], w=okeys(tk0, 512), slot=f"outw{oq}")
                P.barrier()
                P.emit()

        def final_norm_phase():
            with ExitStack() as ns:
                alloc_norm(ns, "fin", 2)
                onc = [ns.enter_context(nc.sbuf_tensor(f"onc{q}", [128, KC, NT], F32)) for q in range(2)]
                for qi, t0 in enumerate(range(0, S, NT)):
                    oq = qi % 2
                    norm_tokens(t0, False, 3, (lambda oq=oq: onc[oq][:]), ("onc", oq), False)
                    P.add("sp", lambda e, t0=t0, oq=oq: e.dma_start(out=out[:, :, t0:t0 + NT], in_=onc[oq][:]), r=[("onc", oq)],
                          w=okeys(t0, NT), slot=f"outf{oq}")
                P.barrier()
                P.emit()

        def mixer_phase():
            with ExitStack() as ms:
                msb = lambda name, shape, dt=F32: ms.enter_context(nc.sbuf_tensor("m_" + name, list(shape), dt))
                SCALE = 128 ** -0.5
                cst = msb("cst", [128, 4, 128]); ident = cst[:, 0, :]; Umat = cst[:, 1, :]; mneg = cst[:, 2, :]; strict = cst[:, 3, :]
                dnw = msb("dnw", [128, 128]); hp = msb("hp", [128, 2, NHD]); convw = msb("convw", [128, 3 * NHD, 4])
                Aexp = msb("Aexp", [128, NHD])
                P.add("sp", lambda e: e.dma_start(out=cst[:], in_=cst_d), w=["cst"], slot="mc0")
                P.add("sp", lambda e: e.dma_start(out=dnw[:], in_=dnw_d), w=["dnw"], slot="mc2")
                P.add("sp", lambda e: e.dma_start(out=hp[:], in_=hp_d), w=["hp"], slot="mc3")
                P.add("sp", lambda e: e.dma_start(out=convw[:], in_=convw_d), w=["convw"], slot="mc4")
                P.add("act", lambda e: e.activation(out=Aexp[:], in_=hp[:, 0, :], func=AF.Exp), r=["hp"], w=["Aexp"])
                with ExitStack() as m0s:
                    alloc_norm(m0s, "m0", 2)
                    hnc = [m0s.enter_context(nc.sbuf_tensor(f"m_hnc{q}", [128, KC, NT], BF16)) for q in range(2)]
                    for qi, t0 in enumerate(range(0, S, NT)):
                        oq = qi % 2
                        norm_tokens(t0, False, 1, (lambda oq=oq: hnc[oq][:]), ("hnc", oq), True)
                        P.add("sp", lambda e, t0=t0, oq=oq: e.dma_start(out=scr_hn[:, :, t0:t0 + NT], in_=hnc[oq][:]), r=[("hnc", oq)],
                              w=[("scr_hn", t0 // TT)], slot=f"shn{oq}")
                    P.barrier()
                    P.emit()
                hnt = [msb(f"hnt{q}", [128, KC, TT], BF16) for q in range(2)]
                wch = [msb(f"wch{q}", [128, KC, 128], BF16) for q in range(4)]
                wba = msb("wba", [128, KC, 2 * NHD], BF16)
                dbda = msb("dbda", [128, NTILE, 2 * NHD])
                astk = ExitStack()
                asb = lambda name, shape, dt=F32: astk.enter_context(nc.sbuf_tensor("a_" + name, list(shape), dt))
                QT2 = [asb(f"QT{q}", [128, S], BF16) for q in range(2)]; KT2 = [asb(f"KT{q}", [128, S], BF16) for q in range(2)]
                V2 = [asb(f"V{q}", [128, NTILE, 128], BF16) for q in range(2)]
                Eb = [asb(f"Eb{q}", [128, 512], BF16) for q in range(2)]
                Pb = [asb(f"Pb{q}", [128, 512], BF16) for q in range(2)]
                rec = asb("rec", [128, 512]); attn = asb("attn", [128, 512], BF16)
                amask = asb("amask", [128, cfg.NOFF * 128], BF16)
                ones_b = asb("ones_b", [128, 128], BF16)
                identb = asb("identb", [128, 128], BF16)
                VT2 = [asb(f"VT{q}", [128, S], BF16) for q in range(2)]
                P.add("act", lambda e: e.copy(out=identb[:], in_=ident), r=["cst"], w=["identb"])
                P.add("pool", lambda e: e.dma_start(out=amask[:], in_=amask_d), w=["amask"], slot="mc1")
                P.add("pool", lambda e: e.memset(ones_b[:], 1.0), w=["ones_b"])
                hcnt = [0]

                def load_hn(g):
                    q = hcnt[0] % 2; hcnt[0] += 1
                    P.add("sp", lambda e: e.dma_start(out=hnt[q][:], in_=scr_hn[:, :, g * TT:(g + 1) * TT]),
                          r=[("scr_hn", g)], w=[("hnt", q)], slot=f"hnt{q}")
                    return q

                def load_wcol(slot_i, col0):
                    src = win[:, col0:col0 + 128].rearrange("(k p) n -> p k n", p=128)
                    P.add("pool", lambda e: e.dma_start(out=wch[slot_i][:], in_=src), w=[("wch", slot_i)], slot=f"wch{slot_i}")

                def fm_proj(slot_i, q, pb, evac):
                    for k in range(KC):
                        P.add("pe", lambda e, k=k: e.matmul(psum[pb][:, 0:TT], lhsT=wch[slot_i][:, k, :], rhs=hnt[q][:, k, :],
                                                            start=(k == 0), stop=(k == KC - 1)),
                              r=[("wch", slot_i), ("hnt", q)], w=[("ps", pb)])
                    evac(pb)

                def tm_proj(w_ap_fn, wkey, q, blk, pb, ncols, evac):
                    for k in range(KC):
                        P.add("pe", lambda e, k=k: e.matmul(psum[pb][:, 0:ncols], lhsT=hnt[q][:, k, blk * 128:(blk + 1) * 128],
                                                            rhs=w_ap_fn(k), start=(k == 0), stop=(k == KC - 1)),
                              r=[wkey, ("hnt", q)], w=[("ps", pb)])
                    evac(pb)

                base_ba = 3 * DA + 4 * DD
                P.add("pool", lambda e: e.dma_start(out=wba[:], in_=win[:, base_ba:base_ba + 2 * NHD].rearrange("(k p) n -> p k n", p=128)),
                      w=["wba"], slot="wba")
                for g in range(NTT):
                    q = load_hn(g)
                    for blk in range(TT // 128):
                        n = g * (TT // 128) + blk
                        tm_proj(lambda k: wba[:, k, :], "wba", q, blk, 6, 2 * NHD,
                                lambda pb, n=n: P.add("act", lambda e: e.copy(out=dbda[:, n, :], in_=psum[pb][:, 0:2 * NHD]),
                                                      r=[("ps", pb)], w=[("dbda", n)]))

                NG = NTT
                NHA_ = NHA if cfg.mixstop >= 1 else 0
                TPG = TT // 128
                for a in range(NHA_):
                    ap_ = a % 2; QT, KT, V, VT = QT2[ap_], KT2[ap_], V2[ap_], VT2[ap_]
                    load_wcol(0, a * 128); load_wcol(1, DA + a * 128); load_wcol(2, 2 * DA + a * 128)
                    for g in range(NTT):
                        q = load_hn(g)
                        fm_proj(0, q, 0, lambda pb, g=g: P.add("act", lambda e, QT=QT: e.copy(out=QT[:, g * TT:(g + 1) * TT], in_=psum[pb][:, 0:TT]),
                                                               r=[("ps", pb)], w=[("QT", ap_, g)]))
                        fm_proj(1, q, 1, lambda pb, g=g: P.add("dve", lambda e, KT=KT: e.tensor_copy(out=KT[:, g * TT:(g + 1) * TT], in_=psum[pb][:, 0:TT]),
                                                               r=[("ps", pb)], w=[("KT", ap_, g)]))
                        fm_proj(2, q, 2, lambda pb, g=g, VT=VT: P.add("act", lambda e: e.copy(out=VT[:, g * TT:(g + 1) * TT], in_=psum[pb][:, 0:TT]),
                                                                      r=[("ps", pb)], w=[("VT", ap_, g)]))
                        for blk in range(TPG):
                            n = g * TPG + blk
                            P.add("pe", lambda e, blk=blk, n=n, VT=VT: e.matmul(psum[3][:, blk * 128:(blk + 1) * 128], lhsT=VT[:, n * 128:(n + 1) * 128],
                                                                                rhs=identb[:], start=True, stop=True),
                                  r=[("VT", ap_, g), "identb"], w=[("ps", 3)])
                        P.add("dve", lambda e, g=g, V=V: e.tensor_copy(out=V[:, g * TPG:(g + 1) * TPG, :].rearrange("p n d -> p (n d)"), in_=psum[3][:, 0:TT]),
                              r=[("ps", 3)], w=[("V", ap_, g * TPG + b_) for b_ in range(TPG)])
                    ec = 0
                    for G in range(NG):
                        mlist = list(range(0, TPG * G + TPG))
                        for mi, m in enumerate(mlist):
                            sb_ = 4 + (ec % 2); eb = ec % 2; ec += 1
                            P.add("pe", lambda e, m=m, G=G, sb_=sb_, KT=KT, QT=QT: e.matmul(psum[sb_][:, 0:TT], lhsT=KT[:, m * 128:(m + 1) * 128],
                                                                              rhs=QT[:, G * TT:(G + 1) * TT], start=True, stop=True),
                                  r=[("KT", ap_, m // TPG), ("QT", ap_, G)], w=[("ps", sb_)])
                            P.add("act", lambda e, sb_=sb_, eb=eb: e.activation(out=Eb[eb][:], in_=psum[sb_][:, 0:TT], func=AF.Exp, scale=SCALE),
                                  r=[("ps", sb_)], w=[("Eb", eb)])
                            o0 = (TPG * G - m + 3) * 128
                            P.add("dve", lambda e, eb=eb, o0=o0: e.tensor_tensor(out=Pb[eb][:], in0=Eb[eb][:], in1=amask[:, o0:o0 + TT], op=ALU.mult),
                                  r=[("Eb", eb), "amask"], w=[("Pb", eb)])
                            P.add("pe", lambda e, m=m, eb=eb, mi=mi, L=len(mlist), V=V: e.matmul(psum[0][:, 0:TT], lhsT=V[:, m, :], rhs=Pb[eb][:],
                                                                                         start=(mi == 0), stop=(mi == L - 1)),
                                  r=[("V", ap_, m), ("Pb", eb)], w=[("ps", 0)])
                            P.add("pe", lambda e, eb=eb, mi=mi, L=len(mlist): e.matmul(psum[1][:, 0:TT], lhsT=ones_b[:], rhs=Pb[eb][:],
                                                                                    start=(mi == 0), stop=(mi == L - 1)),
                                  r=["ones_b", ("Pb", eb)], w=[("ps", 1)])
                        P.add("dve", lambda e: e.reciprocal(out=rec[:], in_=psum[1][:, 0:TT]), r=[("ps", 1)], w=["rec"])
                        P.add("dve", lambda e: e.tensor_tensor(out=attn[:], in0=psum[0][:, 0:TT], in1=rec[:], op=ALU.mult),
                              r=[("ps", 0), "rec"], w=["attn"])
                        P.add("sp", lambda e, a=a, G=G: e.dma_start(out=scr_mix[:, a, G * TT:(G + 1) * TT], in_=attn[:]),
                              r=["attn"], w=[("scr_mix", G)], slot="smx")

                P.barrier()
                P.emit()
                astk.close()
                dstk = ExitStack()
                dsb = lambda name, shape, dt=F32: dstk.enter_context(nc.sbuf_tensor("d_" + name, list(shape), dt))
                pre = [dsb(f"pre{x}", [128, 3 + S]) for x in range(3)]
                cx = [dsb(f"cx{x}", [128, S]) for x in range(3)]
                ZsT = dsb("ZsT", [128, S])
                mixh = dsb("mixh", [128, S], BF16)
                sm = lambda name, shape=(128, 128), dt=F32: dsb(name, list(shape), dt)
                beta = sm("beta", (128, NTILE)); e1 = sm("e1", (128, NTILE)); gg = sm("gg", (128, NTILE))
                gc = sm("gc", (128, NTILE)); ngc = sm("ngc", (128, NTILE)); egc = sm("egc", (128, NTILE)); negc = sm("negc", (128, NTILE))
                gl = sm("gl", (128, NTILE)); egl = sm("egl", (128, NTILE)); ekd = sm("ekd", (128, NTILE))
                sso = sm("sso", (128, 1)); ro = sm("ro", (128, 1))
                Ysb = sm("Ysb"); oA = sm("oA"); vnew = sm("vnew"); osb = sm("osb"); ot = sm("ot"); St = sm("St")
                bsets = []
                for si, banks in enumerate(((2, 3), (0, 1))):
                    B = {"si": si, "banks": banks, "ssq": sm(f"ssq{si}", (128, 2)), "rqk": sm(f"rqk{si}", (128, 2))}
                    for nm in ("tq", "tk", "qn", "kn", "dg", "decT", "Nm", "Mm", "Mt"):
                        B[nm] = sm(f"{nm}{si}")
                    for nm in ("vt", "knT", "qnT", "kdec", "PT", "QKd"):
                        B[nm] = [sm(f"{nm}{si}_{par}") for par in range(2)]
                    bsets.append(B)
                for x in range(3):
                    P.add("pool", lambda e, x=x: e.memset(pre[x][:, 0:3], 0.0), w=[("pre", x)])

                def pq(b, qd):
                    return psum[b][:, qd * 128:(qd + 1) * 128]

                for h in range(NHD if cfg.mixstop >= 2 else 0):
                    cols = (3 * DA + h * 128, 3 * DA + DD + h * 128, 3 * DA + 2 * DD + h * 128, 3 * DA + 3 * DD + h * 128)
                    for x in range(4):
                        load_wcol(x, cols[x])
                    for g in range(NTT):
                        q = load_hn(g)
                        for x in range(3):
                            fm_proj(x, q, (4, 5, 4)[x], lambda pb, x=x, g=g: P.add(
                                "act" if x != 1 else "dve",
                                (lambda e: e.copy(out=pre[x][:, 3 + g * TT:3 + (g + 1) * TT], in_=psum[pb][:, 0:TT])) if x != 1 else
                                (lambda e: e.tensor_copy(out=pre[x][:, 3 + g * TT:3 + (g + 1) * TT], in_=psum[pb][:, 0:TT])),
                                r=[("ps", pb)], w=[("pre", x)]))
                        fm_proj(3, q, 5, lambda pb, g=g: P.add("act", lambda e: e.activation(out=ZsT[:, g * TT:(g + 1) * TT], in_=psum[pb][:, 0:TT], func=AF.Silu),
                                                               r=[("ps", pb)], w=[("ZsT", g)]))
                    dk_ = [("dbda", n) for n in range(NTILE)]
                    P.add("act", lambda e, h=h: e.activation(out=beta[:], in_=dbda[:, :, h], func=AF.Sigmoid), r=dk_, w=["beta"])
                    P.add("act", lambda e, h=h: e.activation(out=e1[:], in_=dbda[:, :, NHD + h], func=AF.Exp, bias=hp[:, 1, h:h + 1]),
                          r=dk_ + ["hp"], w=["e1"])
                    P.add("act", lambda e: e.activation(out=e1[:], in_=e1[:], func=AF.Ln, bias=1.0), r=["e1"], w=["e1"])
                    P.add("dve", lambda e, h=h: e.tensor_scalar(out=gg[:], in0=e1[:], scalar1=Aexp[:, h:h + 1], scalar2=-1.0,
                                                                op0=ALU.mult, op1=ALU.mult), r=["e1", "Aexp"], w=["gg"])
                    P.add("pe", lambda e: e.matmul(pq(7, 0)[:, 0:NTILE], lhsT=Umat, rhs=gg[:], start=True, stop=True),
                          r=["cst", "gg"], w=[("ps", 7)])
                    P.add("pe", lambda e: e.matmul(pq(7, 1)[:, 0:NTILE], lhsT=ones_f[:], rhs=gg[:], start=True, stop=True),
                          r=["ones_f", "gg"], w=[("ps", 7)])
                    P.add("dve", lambda e: e.tensor_copy(out=gc[:], in_=pq(7, 0)[:, 0:NTILE]), r=[("ps", 7)], w=["gc"])
                    P.add("dve", lambda e: e.tensor_copy(out=gl[:], in_=pq(7, 1)[:, 0:NTILE]), r=[("ps", 7)], w=["gl"])
                    P.add("dve", lambda e: e.tensor_scalar_mul(out=ngc[:], in0=gc[:], scalar1=-1.0), r=["gc"], w=["ngc"])
                    P.add("act", lambda e: e.activation(out=egc[:], in_=gc[:], func=AF.Exp), r=["gc"], w=["egc"])
                    P.add("dve", lambda e: e.tensor_scalar_mul(out=negc[:], in0=egc[:], scalar1=-1.0), r=["egc"], w=["negc"])
                    P.add("act", lambda e: e.activation(out=egl[:], in_=gl[:], func=AF.Exp), r=["gl"], w=["egl"])
                    P.add("dve", lambda e: e.tensor_tensor(out=ekd[:], in0=gl[:], in1=gc[:], op=ALU.subtract), r=["gl", "gc"], w=["ekd"])
                    P.add("act", lambda e: e.activation(out=ekd[:], in_=ekd[:], func=AF.Exp), r=["ekd"], w=["ekd"])
                    for x in range(3):
                        ch = x * NHD + h
                        P.add("dve", lambda e, x=x, ch=ch: e.tensor_scalar_mul(out=cx[x][:], in0=pre[x][:, 0:S], scalar1=convw[:, ch, 0:1]),
                              r=[("pre", x), "convw"], w=[("cx", x)])
                        for i in range(1, 4):
                            P.add("dve", lambda e, x=x, ch=ch, i=i: e.scalar_tensor_tensor(
                                out=cx[x][:], in0=pre[x][:, i:i + S], scalar=convw[:, ch, i:i + 1], in1=cx[x][:], op0=ALU.mult, op1=ALU.add),
                                r=[("pre", x), "convw", ("cx", x)], w=[("cx", x)])
                        P.add("act", lambda e, x=x: e.activation(out=cx[x][:], in_=cx[x][:], func=AF.Silu), r=[("cx", x)], w=[("cx", x)])
                    P.add("pool", lambda e: e.memset(St[:], 0.0), w=["St"])
                    OUTS = ("vt", "knT", "qnT", "kdec", "PT", "QKd")

                    def prep(n, B, par):
                        si = B["si"]; b0, b1 = B["banks"]
                        K_ = lambda nm: (nm, si, par) if nm in OUTS else (nm, si)
                        ts = slice(n * 128, (n + 1) * 128); nn = slice(n, n + 1)
                        tq, tk, ssq, rqk, qn, kn, dg, decT, Nm, Mm, Mt = (B[k] for k in (
                            "tq", "tk", "ssq", "rqk", "qn", "kn", "dg", "decT", "Nm", "Mm", "Mt"))
                        vt, knT, qnT, kdec, PT, QKd = (B[k][par] for k in OUTS)
                        for x in range(3):
                            P.add("pe", lambda e, x=x: e.matmul(pq(b0, x), lhsT=cx[x][:, ts], rhs=ident, start=True, stop=True),
                                  r=[("cx", x), "cst"], w=[("ps", b0)])
                        yield
                        P.add("act", lambda e: e.activation(out=tq[:], in_=pq(b0, 0), func=AF.Square), r=[("ps", b0)], w=[K_("tq")]); yield
                        P.add("act", lambda e: e.activation(out=tk[:], in_=pq(b0, 1), func=AF.Square), r=[("ps", b0)], w=[K_("tk")]); yield
                        P.add("act", lambda e: e.copy(out=vt[:], in_=pq(b0, 2)), r=[("ps", b0)], w=[K_("vt")]); yield
                        P.add("dve", lambda e: e.reduce_sum(out=ssq[:, 0:1], in_=tq[:], axis=mybir.AxisListType.X), r=[K_("tq")], w=[K_("ssq")]); yield
                        P.add("dve", lambda e: e.reduce_sum(out=ssq[:, 1:2], in_=tk[:], axis=mybir.AxisListType.X), r=[K_("tk"), K_("ssq")], w=[K_("ssq")]); yield
                        P.add("act", lambda e: e.activation(out=rqk[:], in_=ssq[:], func=AF.Sqrt, bias=EPS), r=[K_("ssq")], w=[K_("rqk")]); yield
                        P.add("dve", lambda e: e.reciprocal(out=rqk[:], in_=rqk[:]), r=[K_("rqk")], w=[K_("rqk")]); yield
                        P.add("dve", lambda e: e.tensor_scalar(out=qn[:], in0=pq(b0, 0), scalar1=rqk[:, 0:1], scalar2=SCALE,
                                                               op0=ALU.mult, op1=ALU.mult), r=[("ps", b0), K_("rqk")], w=[K_("qn")]); yield
                        P.add("dve", lambda e: e.tensor_scalar_mul(out=kn[:], in0=pq(b0, 1), scalar1=rqk[:, 1:2]), r=[("ps", b0), K_("rqk")], w=[K_("kn")]); yield
                        P.add("pe", lambda e: e.matmul(pq(b1, 0), lhsT=kn[:], rhs=ident, start=True, stop=True), r=[K_("kn"), "cst"], w=[("ps", b1)])
                        P.add("pe", lambda e: e.matmul(pq(b1, 1), lhsT=qn[:], rhs=ident, start=True, stop=True), r=[K_("qn"), "cst"], w=[("ps", b1)]); yield
                        P.add("dve", lambda e: e.tensor_copy(out=knT[:], in_=pq(b1, 0)), r=[("ps", b1)], w=[K_("knT")]); yield
                        P.add("dve", lambda e: e.tensor_copy(out=qnT[:], in_=pq(b1, 1)), r=[("ps", b1)], w=[K_("qnT")]); yield
                        P.add("dve", lambda e: e.tensor_scalar_mul(out=kdec[:], in0=kn[:], scalar1=ekd[:, nn]), r=[K_("kn"), "ekd"], w=[K_("kdec")]); yield
                        P.add("dve", lambda e: e.tensor_scalar_mul(out=dg[:], in0=ident, scalar1=gc[:, nn]), r=["cst", "gc"], w=[K_("dg")]); yield
                        P.add("pe", lambda e: e.matmul(pq(b0, 3), lhsT=ones_f[:], rhs=dg[:], start=True, stop=False), r=["ones_f", K_("dg")], w=[("ps", b0)])
                        P.add("pe", lambda e: e.matmul(pq(b0, 3), lhsT=ident, rhs=mneg, start=False, stop=True), r=["cst"], w=[("ps", b0)])
                        P.add("pe", lambda e: e.matmul(pq(b1, 2), lhsT=knT[:], rhs=knT[:], start=True, stop=True), r=[K_("knT")], w=[("ps", b1)])
                        P.add("pe", lambda e: e.matmul(pq(b1, 3), lhsT=knT[:], rhs=qnT[:], start=True, stop=True), r=[K_("knT"), K_("qnT")], w=[("ps", b1)]); yield
                        P.add("act", lambda e: e.activation(out=decT[:], in_=pq(b0, 3), func=AF.Exp, bias=ngc[:, nn]), r=[("ps", b0), "ngc"], w=[K_("decT")]); yield
                        P.add("dve", lambda e: e.scalar_tensor_tensor(out=Nm[:], in0=pq(b1, 2), scalar=beta[:, nn], in1=decT[:],
                                                                      op0=ALU.mult, op1=ALU.mult), r=[("ps", b1), "beta", K_("decT")], w=[K_("Nm")]); yield
                        P.add("dve", lambda e: e.tensor_tensor(out=QKd[:], in0=pq(b1, 3), in1=decT[:], op=ALU.mult), r=[("ps", b1), K_("decT")], w=[K_("QKd")]); yield
                        P.add("dve", lambda e: e.tensor_tensor(out=Nm[:], in0=Nm[:], in1=strict, op=ALU.mult), r=[K_("Nm"), "cst"], w=[K_("Nm")]); yield
                        P.add("pe", lambda e: e.matmul(pq(b1, 0), lhsT=Nm[:], rhs=ident, start=True, stop=True), r=[K_("Nm"), "cst"], w=[("ps", b1)]); yield
                        P.add("dve", lambda e: e.tensor_copy(out=Mt[:], in_=pq(b1, 0)), r=[("ps", b1)], w=[K_("Mt")]); yield
                        P.add("dve", lambda e: e.tensor_tensor(out=PT[:], in0=ident, in1=Nm[:], op=ALU.subtract), r=["cst", K_("Nm")], w=[K_("PT")]); yield
                        P.add("act", lambda e: e.copy(out=Mm[:], in_=Nm[:]), r=[K_("Nm")], w=[K_("Mm")]); yield
                        for st in range(6):
                            if st < 5:
                                P.add("pe", lambda e: e.matmul(pq(b0, 0), lhsT=Mt[:], rhs=Mm[:], start=True, stop=True), r=[K_("Mt"), K_("Mm")], w=[("ps", b0)])
                            P.add("pe", lambda e: e.matmul(pq(b1, 1), lhsT=Mm[:], rhs=Mt[:], start=True, stop=True), r=[K_("Mt"), K_("Mm")], w=[("ps", b1)]); yield
                            if st < 5:
                                P.add("act", lambda e: e.copy(out=Mm[:], in_=pq(b0, 0)), r=[("ps", b0)], w=[K_("Mm")]); yield
                            P.add("dve", lambda e: e.tensor_copy(out=Mt[:], in_=pq(b1, 1)), r=[("ps", b1)], w=[K_("Mt")]); yield
                            P.add("pe", lambda e: e.matmul(pq(b1, 2), lhsT=Mt[:], rhs=PT[:], start=True, stop=True), r=[K_("Mt"), K_("PT")], w=[("ps", b1)]); yield
                            P.add("dve", lambda e: e.tensor_tensor(out=PT[:], in0=PT[:], in1=pq(b1, 2), op=ALU.add), r=[K_("PT"), ("ps", b1)], w=[K_("PT")]); yield

                    def rec(n, B, par):
                        si = B["si"]
                        K_ = lambda nm: (nm, si, par)
                        ts = slice(n * 128, (n + 1) * 128); nn = slice(n, n + 1)
                        vt, knT, qnT, kdec, PT, QKd = (B[k][par] for k in OUTS)
                        P.add("pe", lambda e: e.matmul(pq(7, 0), lhsT=knT[:], rhs=St[:], start=True, stop=True), r=[K_("knT"), "St"], w=[("ps", 7)])
                        yield
                        P.add("pe", lambda e: e.matmul(pq(7, 1), lhsT=qnT[:], rhs=St[:], start=True, stop=True), r=[K_("qnT"), "St"], w=[("ps", 7)])
                        yield
                        P.add("dve", lambda e: e.scalar_tensor_tensor(out=Ysb[:], in0=pq(7, 0), scalar=negc[:, nn], in1=vt[:],
                                                                      op0=ALU.mult, op1=ALU.add), r=[("ps", 7), "negc", K_("vt")], w=["Ysb"])
                        yield
                        P.add("dve", lambda e: e.tensor_scalar_mul(out=oA[:], in0=pq(7, 1), scalar1=egc[:, nn]), r=[("ps", 7), "egc"], w=["oA"])
                        yield
                        P.add("pe", lambda e: e.matmul(pq(7, 2), lhsT=PT[:], rhs=Ysb[:], start=True, stop=True), r=[K_("PT"), "Ysb"], w=[("ps", 7)])
                        yield
                        P.add("dve", lambda e: e.tensor_scalar_mul(out=vnew[:], in0=pq(7, 2), scalar1=beta[:, nn]), r=[("ps", 7), "beta"], w=["vnew"])
                        yield
                        P.add("pe", lambda e: e.matmul(pq(7, 3), lhsT=QKd[:], rhs=vnew[:], start=True, stop=True), r=[K_("QKd"), "vnew"], w=[("ps", 7)])
                        yield
                        P.add("pe", lambda e: e.matmul(pq(7, 0), lhsT=kdec[:], rhs=vnew[:], start=True, stop=True), r=[K_("kdec"), "vnew"], w=[("ps", 7)])
                        yield
                        P.add("dve", lambda e: e.tensor_tensor(out=osb[:], in0=pq(7, 3), in1=oA[:], op=ALU.add), r=[("ps", 7), "oA"], w=["osb"])
                        yield
                        P.add("dve", lambda e: e.scalar_tensor_tensor(out=St[:], in0=St[:], scalar=egl[:, nn], in1=pq(7, 0),
                                                                      op0=ALU.mult, op1=ALU.add), r=["St", "egl", ("ps", 7)], w=["St"])
                        yield
                        P.add("act", lambda e: e.activation(out=ot[:], in_=osb[:], func=AF.Square), r=["osb"], w=["ot"])
                        yield
                        P.add("dve", lambda e: e.reduce_sum(out=sso[:], in_=ot[:], axis=mybir.AxisListType.X), r=["ot"], w=["sso"])
                        yield
                        P.add("act", lambda e: e.activation(out=ro[:], in_=sso[:], func=AF.Sqrt, bias=EPS, scale=1.0 / 128), r=["sso"], w=["ro"])
                        yield
                        P.add("dve", lambda e: e.reciprocal(out=ro[:], in_=ro[:]), r=["ro"], w=["ro"])
                        yield
                        P.add("dve", lambda e: e.scalar_tensor_tensor(out=ot[:], in0=osb[:], scalar=ro[:, 0:1], in1=dnw[:], op0=ALU.mult, op1=ALU.mult),
                              r=["osb", "ro", "dnw"], w=["ot"])
                        yield
                        P.add("pe", lambda e: e.matmul(pq(6, 0), lhsT=ot[:], rhs=ident, start=True, stop=True), r=["ot", "cst"], w=[("ps", 6)])
                        yield
                        P.add("dve", lambda e: e.tensor_tensor(out=mixh[:, ts], in0=pq(6, 0), in1=ZsT[:, ts], op=ALU.mult),
                              r=[("ps", 6), ("ZsT", n // TPG)], w=["mixh"])
                        yield

                    def rec_pair(n0, par):
                        for d in range(2):
                            if n0 + d < NTILE:
                                yield from rec(n0 + d, bsets[d], par)

                    def run_rr(gens):
                        live = list(gens)
                        while live:
                            for g_ in list(live):
                                try:
                                    next(g_)
                                except StopIteration:
                                    live.remove(g_)

                    pending = None
                    for pi_, n0 in enumerate(range(0, NTILE, 2)):
                        par = pi_ % 2
                        gens = [prep(n0 + d, bsets[d], par) for d in range(2) if n0 + d < NTILE]
                        if pending is not None:
                            gens.append(pending)
                        run_rr(gens)
                        pending = rec_pair(n0, par)
                    run_rr([pending])
                    P.add("sp", lambda e, h=h: e.dma_start(out=scr_mix[:, NHA + h, :], in_=mixh[:]), r=["mixh"],
                          w=[("scr_mix", G) for G in range(NTT)], slot="smx2")

                P.barrier()
                P.emit()
                dstk.close()
                mxt = list(hnt) + [msb(f"mxt{q}", [128, KC, TT], BF16) for q in range(max(0, NTT - 2))]
                wot = [wch[0], wch[1]]
                oldm = [msb(f"oldm{q}", [128, TT]) for q in range(2)]
                newm = [msb(f"newm{q}", [128, TT]) for q in range(2)]
                ngs = NTT if cfg.mixstop >= 3 else 0
                for g in range(ngs):
                    P.add("sp", lambda e, g=g: e.dma_start(out=mxt[g][:, 0:KM, :], in_=scr_mix[:, :, g * TT:(g + 1) * TT]),
                          r=[("scr_mix", g)], w=[("hnt", g % 2) if g < 2 else ("mxt", g)], slot=f"mxl{g % 2}")
                wc = 0; oc = 0
                for c in range(KC if ngs else 0):
                    wq = wc % 2; wc += 1
                    src = wo[:, c * 128:(c + 1) * 128].rearrange("(k p) n -> p k n", p=128)
                    P.add("pool", lambda e, wq=wq, src=src: e.dma_start(out=wot[wq][:, 0:KM, :], in_=src), w=[("wch", wq)], slot=f"wch{wq}")
                    for g in range(ngs):
                        oq = oc % 2; oc += 1
                        P.add("sp", lambda e, oq=oq, c=c, g=g: e.dma_start(out=oldm[oq][:], in_=out[:, c, g * TT:(g + 1) * TT]),
                              r=okeys(g * TT, TT), w=[("oldm", oq)], slot=f"oldm{oq}")
                        pb = 4 + (oc % 2)
                        for k in range(KM):
                            P.add("pe", lambda e, pb=pb, wq=wq, k=k, g=g: e.matmul(psum[pb][:, 0:TT], lhsT=wot[wq][:, k, :], rhs=mxt[g][:, k, :],
                                                                                  start=(k == 0), stop=(k == KM - 1)),
                                  r=[("wch", wq), ("hnt", g % 2) if g < 2 else ("mxt", g)], w=[("ps", pb)])
                        P.add("dve", lambda e, pb=pb, oq=oq: e.tensor_tensor(out=newm[oq][:], in0=psum[pb][:, 0:TT], in1=oldm[oq][:], op=ALU.add),
                              r=[("ps", pb), ("oldm", oq)], w=[("newm", oq)])
                        P.add("sp", lambda e, oq=oq, c=c, g=g: e.dma_start(out=out[:, c, g * TT:(g + 1) * TT], in_=newm[oq][:]),
                              r=[("newm", oq)], w=okeys(g * TT, TT), slot=f"outm{oq}")
                P.barrier()
                P.emit()

        from_x = True
        for ph in phases:
            if ph == "ffn1":
                ffn_phase(0, 0, from_x)
            elif ph == "ffn2":
                ffn_phase(1, 2, from_x)
            elif ph == "mix":
                if from_x:
                    raise ValueError("mixer phase needs the residual in `out` (run an FFN phase first)")
                mixer_phase()
            from_x = False
        final_norm_phase()
        P.barrier()
        P.emit()
    return nc


def _fm(a):
    t, d = a.shape
    return np.ascontiguousarray(a.reshape(t, d // 128, 128).transpose(2, 1, 0))


def _amask(cfg):
    def mult(dl):
        return ((dl >= 0) & (dl <= 128)).astype(np.float32) + ((dl >= 0) & (dl <= 512) & (dl % 4 == 0)) + \
               ((dl >= 0) & (dl <= 2048) & (dl % 16 == 0))
    k = np.arange(128)[:, None]
    qq = np.arange(128)[None, :]
    tiles = [mult(off * 128 + qq - k) for off in range(-3, cfg.NTILE)]
    return np.ascontiguousarray(np.concatenate(tiles, axis=1), dtype=np.float32)


def _cst():
    j = np.arange(128)[:, None]
    i = np.arange(128)[None, :]
    ident = (i == j).astype(np.float32)
    U = (j <= i).astype(np.float32)
    mneg = np.where(i >= j, 0.0, -30000.0).astype(np.float32)
    strict = (i > j).astype(np.float32)
    return np.ascontiguousarray(np.stack([ident, U, mneg, strict], axis=1))


def prep_inputs(cfg, inputs):
    D, KC, B, NHD = cfg.D, cfg.KC, cfg.B, cfg.NHD
    f32 = lambda a: np.ascontiguousarray(np.asarray(a), dtype=np.float32)
    x = f32(inputs["x"])
    nrm = np.ascontiguousarray(np.stack([f32(inputs[k]).reshape(KC, 128).T for k in
                                         ("ffn1_norm", "mix_norm", "ffn2_norm", "final_norm")], axis=1))
    shared = {"norms": nrm}
    for f, pre in ((1, "ffn1"), (2, "ffn2")):
        shared[f"wg{f}"] = f32(inputs[pre + "_w_gate"]); shared[f"wu{f}"] = f32(inputs[pre + "_w_up"])
        shared[f"wd{f}"] = f32(inputs[pre + "_w_down"])
    shared["win"] = f32(inputs["w_in"]); shared["wo"] = f32(inputs["w_out"])
    cw = f32(inputs["conv_w"])
    shared["convw"] = np.ascontiguousarray(cw.reshape(4, 3 * NHD, 128).transpose(2, 1, 0))
    shared["hp"] = np.ascontiguousarray(np.broadcast_to(np.stack([f32(inputs["a_log"]), f32(inputs["dt_bias"])])[None], (128, 2, NHD)))
    shared["dnw"] = np.ascontiguousarray(np.broadcast_to(f32(inputs["dn_norm"])[None, :], (128, 128)))
    shared["cst"] = _cst(); shared["amask"] = _amask(cfg)
    maps = []
    for b in range(B):
        m = dict(shared)
        m["xT"] = _fm(x[b])
        maps.append(m)
    return maps


def assemble(cfg, results):
    out = np.empty((cfg.B, cfg.S, cfg.D), np.float32)
    for b in range(cfg.B):
        o = np.asarray(results[b]["out"]).reshape(128, cfg.KC, cfg.S)
        out[b] = o.transpose(2, 1, 0).reshape(cfg.S, cfg.D)
    return out


def kernel(**inputs):
    cfg = Cfg()
    nc = build(cfg)
    maps = prep_inputs(cfg, inputs)
    res = run_bass_kernel_spmd(nc, maps, core_ids=list(range(cfg.B)))
    return assemble(cfg, res.results)
```

```python
import numpy as np
import ml_dtypes
import concourse.bass as bass
import concourse.mybir as mybir
from concourse.bass_utils import run_bass_kernel_spmd

F32 = mybir.dt.float32
BF16 = mybir.dt.bfloat16
ALU = mybir.AluOpType
AF = mybir.ActivationFunctionType
NCORE = 8
EPS = 1e-6


class Op:
    __slots__ = ("eng", "fn", "deps", "kind", "slot", "val", "sig", "ord", "idx")

    def __init__(self, eng, fn, kind):
        self.eng = eng; self.fn = fn; self.kind = kind
        self.deps = []; self.slot = None; self.val = 0; self.sig = False; self.ord = 0


class Prog:
    ENGS = ("pe", "act", "dve", "pool", "sp")

    def __init__(self, nc, stack):
        self.nc = nc
        self.stack = stack
        self.pending = {e: [] for e in self.ENGS}
        self.last_w = {}
        self.readers = {}
        self.slot_cnt = {}
        self.slot_sem = {}
        self.eng_sem = {e: stack.enter_context(nc.semaphore("es_" + e)) for e in ("pe", "act", "dve", "pool")}
        self.eng_cnt = {e: 0 for e in self.ENGS}
        self.waited = {e: {} for e in self.ENGS}
        self.last_op = {e: None for e in self.ENGS}
        self.open_dmas = []

    def _slot(self, slot):
        if slot not in self.slot_sem:
            self.slot_sem[slot] = self.stack.enter_context(self.nc.semaphore("ds_" + slot))
            self.slot_cnt[slot] = 0
        return self.slot_sem[slot]

    def add(self, eng, fn, r=(), w=(), slot=None, cc=False, extra=()):
        kind = "cc" if cc else ("dma" if slot is not None else "cmp")
        op = Op(eng, fn, kind)
        deps = []
        for k in r:
            lw = self.last_w.get(k)
            if lw is not None:
                deps.append(lw)
            if isinstance(k, tuple) and k[0] == "ps":
                deps.extend(rd for rd in self.readers.get(k, ()) if rd.eng != eng)
        for k in w:
            lw = self.last_w.get(k)
            if lw is not None:
                deps.append(lw)
            deps.extend(self.readers.get(k, ()))
        deps.extend(extra)
        for k in r:
            self.readers.setdefault(k, []).append(op)
        for k in w:
            self.last_w[k] = op
            self.readers[k] = []
        seen = set()
        for d in deps:
            if d is op or id(d) in seen:
                continue
            seen.add(id(d))
            if d.kind == "cmp":
                if d.eng == "pe" and eng == "pe":
                    continue
                d.sig = True
            op.deps.append(d)
        if slot is not None:
            self._slot(slot)
            self.slot_cnt[slot] += 1
            op.slot = slot
            op.val = self.slot_cnt[slot] * (1 if cc else 16)
            self.open_dmas.append(op)
        self.pending[eng].append(op)
        self.last_op[eng] = op
        return op

    def barrier(self):
        lasts = [o for o in self.last_op.values() if o is not None and o.kind == "cmp"]
        dmas = list({o.slot: o for o in self.open_dmas}.values())
        self.open_dmas = []
        for e in self.ENGS:
            self.add(e, None, extra=lasts + dmas)

    def emit(self):
        nc = self.nc
        engs = {"pe": "tensor", "act": "scalar", "dve": "vector", "pool": "gpsimd", "sp": "sync"}
        pend = self.pending
        self.pending = {e: [] for e in self.ENGS}
        for e in self.ENGS:
            for op in pend[e]:
                if op.kind == "cmp" and op.sig:
                    self.eng_cnt[e] += 1
                    op.ord = self.eng_cnt[e]
        with nc.Block() as block:
            for e in self.ENGS:
                ops = pend[e]
                if not ops:
                    continue

                def section(engine, ops=ops, e=e):
                    waited = self.waited[e]
                    for op in ops:
                        need = {}
                        for d in op.deps:
                            if d.kind == "cmp":
                                key = ("e", d.eng); v = d.ord
                            else:
                                key = ("s", d.slot); v = d.val
                            if v > need.get(key, 0):
                                need[key] = v
                        for key, v in need.items():
                            if waited.get(key, 0) >= v:
                                continue
                            sem = self.eng_sem[key[1]] if key[0] == "e" else self.slot_sem[key[1]]
                            engine.wait_ge(sem, v)
                            waited[key] = v
                        if op.fn is None:
                            continue
                        ins = op.fn(engine)
                        if op.kind == "cmp":
                            if op.sig:
                                ins.then_inc(self.eng_sem[e], 1)
                        elif op.kind == "dma":
                            ins.then_inc(self.slot_sem[op.slot], 16)
                        else:
                            ins.then_inc(self.slot_sem[op.slot], 1)

                getattr(block, engs[e])(section)


class Cfg:
    def __init__(self, D=4096, FF=11008, S=2048, B=4, NHA=16, NHD=16):
        self.D = D; self.FF = FF; self.S = S; self.B = B; self.NHA = NHA; self.NHD = NHD
        self.KC = D // 128
        self.NJ = FF // 128
        self.DA = NHA * 128; self.DD = NHD * 128
        self.KM = NHA + NHD
        self.WIN = 3 * self.DA + 4 * self.DD + 2 * NHD
        self.TT = 512
        self.TF = min(1024, S)
        self.NGRP = 4
        self.NTT = S // self.TT
        self.NTILE = S // 128
        self.NT = 64
        self.NOFF = self.NTILE + 3
        self.mixstop = 9


def build(cfg, phases=("ffn1", "mix", "ffn2")):
    from contextlib import ExitStack
    D, KC, FF, NJ, S, TT, NTT, NT, NTILE = cfg.D, cfg.KC, cfg.FF, cfg.NJ, cfg.S, cfg.TT, cfg.NTT, cfg.NT, cfg.NTILE
    NHA, NHD, DA, DD, KM, WIN = cfg.NHA, cfg.NHD, cfg.DA, cfg.DD, cfg.KM, cfg.WIN
    nc = bass.Bass("TRN2", target_bir_lowering=False)
    dt_in = lambda name, shape: nc.dram_tensor(name, list(shape), F32, kind="ExternalInput").ap()
    xT = dt_in("xT", [128, KC, S])
    norms = dt_in("norms", [128, 4, KC])
    wg = [dt_in("wg1", [D, FF]), dt_in("wg2", [D, FF])]
    wu = [dt_in("wu1", [D, FF]), dt_in("wu2", [D, FF])]
    wd = [dt_in("wd1", [FF, D]), dt_in("wd2", [FF, D])]
    win = dt_in("win", [D, WIN])
    convw_d = dt_in("convw", [128, 3 * NHD, 4])
    hp_d = dt_in("hp", [128, 2, NHD])
    dnw_d = dt_in("dnw", [128, 128])
    wo = dt_in("wo", [KM * 128, D])
    cst_d = dt_in("cst", [128, 4, 128])
    amask_d = dt_in("amask", [128, cfg.NOFF * 128])
    out = nc.dram_tensor("out", [128, KC, S], F32, kind="ExternalOutput").ap()
    scr_hn = nc.dram_tensor("scr_hn", [NTT, 128, KC, TT], BF16, kind="ExternalOutput").ap()
    scr_mix = nc.dram_tensor("scr_mix", [128, KM, S], BF16, kind="ExternalOutput").ap()

    with ExitStack() as stack:
        P = Prog(nc, stack)
        sb = lambda name, shape, dt=F32: stack.enter_context(nc.sbuf_tensor(name, list(shape), dt))
        ones_f = sb("ones_f", [128, 128])
        normw = sb("normw", [128, 4, KC])
        psum = [stack.enter_context(nc.psum_tensor(f"ps{i}", [128, 512], F32)) for i in range(8)]
        P.add("pool", lambda e: e.memset(ones_f[:], 1.0), w=["ones_f"])
        P.add("sp", lambda e: e.dma_start(out=normw[:], in_=norms), w=["normw"], slot="const")
        NB = {}

        def alloc_norm(stk, tag, nbuf):
            a = lambda name, shape: stk.enter_context(nc.sbuf_tensor(f"{tag}_{name}", list(shape), F32))
            NB.clear()
            NB.update(n=nbuf, cnt=0, banks=(6, 7) if nbuf > 1 else (6,),
                      hc=[a(f"hc{q}", [128, KC, NT]) for q in range(nbuf)], sq=[a(f"sq{q}", [128, KC, NT]) for q in range(nbuf)],
                      rstd=[a(f"rstd{q}", [128, NT]) for q in range(nbuf)], ssum=[a(f"ssum{q}", [128, NT]) for q in range(nbuf)])

        def okeys(t0, n):
            return [("out", q) for q in range(t0 // NT, (t0 + n + NT - 1) // NT)]

        def norm_tokens(t0, from_x, widx, dst_fn, dst_key, bf):
            srcr = xT if from_x else out
            i = NB["cnt"] % NB["n"]; NB["cnt"] += 1
            hc, sq, rstd, ssum = NB["hc"][i], NB["sq"][i], NB["rstd"][i], NB["ssum"][i]
            pb = NB["banks"][i % len(NB["banks"])]
            hk, sk, rk, mk = ("hc", i), ("sq", i), ("rstd", i), ("ssum", i)
            P.add("sp", lambda e: e.dma_start(out=hc[:], in_=srcr[:, :, t0:t0 + NT]),
                  r=[] if from_x else okeys(t0, NT), w=[hk], slot=f"hc{i}")
            P.add("act", lambda e: e.activation(out=sq[:], in_=hc[:], func=AF.Square), r=[hk], w=[sk])
            P.add("dve", lambda e: e.reduce_sum(out=ssum[:], in_=sq[:].rearrange("p c t -> p t c"), axis=mybir.AxisListType.X),
                  r=[sk], w=[mk])
            P.add("pe", lambda e: e.matmul(psum[pb][:, 0:NT], lhsT=ones_f[:], rhs=ssum[:], start=True, stop=True),
                  r=[mk, "ones_f"], w=[("ps", pb)])
            P.add("act", lambda e: e.activation(out=rstd[:], in_=psum[pb][:, 0:NT], func=AF.Sqrt, bias=EPS, scale=1.0 / D),
                  r=[("ps", pb)], w=[rk])
            P.add("dve", lambda e: e.reciprocal(out=rstd[:], in_=rstd[:]), r=[rk], w=[rk])
            P.add("dve", lambda e: e.tensor_tensor(out=sq[:], in0=hc[:],
                                                   in1=rstd[:].unsqueeze(1).broadcast_to([128, KC, NT]), op=ALU.mult),
                  r=[hk, rk], w=[sk])
            P.add("dve", lambda e: e.tensor_tensor(out=dst_fn(), in0=sq[:],
                                                   in1=normw[:, widx, :].unsqueeze(2).broadcast_to([128, KC, NT]),
                                                   op=ALU.mult), r=[sk, "normw"], w=[dst_key])

        def ffn_phase(f, widx, from_x):
            with ExitStack() as fs:
                fsb = lambda name, shape, dt=F32: fs.enter_context(nc.sbuf_tensor(f"f{f}_" + name, list(shape), dt))
                TF = cfg.TF; NH = TF // 512; NTF = S // TF
                alloc_norm(fs, f"f{f}n", 2)
                groups = [list(g) for g in np.array_split(np.arange(NJ), cfg.NGRP) if len(g)]
                GM = max(len(g) for g in groups)
                hn = fsb("hn", [128, KC, TF], BF16)
                wgt = [fsb(f"wgt{q}", [128, KC, 128], BF16) for q in range(2)]
                wut = [fsb(f"wut{q}", [128, KC, 128], BF16) for q in range(2)]
                hid = fsb("hid", [128, GM, TF], BF16)
                wdt = [fsb(f"wdt{q}", [128, GM, 128], BF16) for q in range(2)]
                sil = [fsb(f"sil{q}", [128, 512]) for q in range(2)]
                oldt = [fsb(f"old{q}", [128, 512]) for q in range(2)]
                newt = [fsb(f"new{q}", [128, 512]) for q in range(2)]
                wcnt = [0]; dcnt = [0]; pcnt = [0]; scnt = [0]; ocnt = [0]; dps = [0]

                def load_w(j):
                    q = wcnt[0] % 2; wcnt[0] += 1
                    srcg = wg[f][:, j * 128:(j + 1) * 128].rearrange("(k p) n -> p k n", p=128)
                    srcu = wu[f][:, j * 128:(j + 1) * 128].rearrange("(k p) n -> p k n", p=128)
                    P.add("pool", lambda e: e.dma_start(out=wgt[q][:], in_=srcg), w=[("wgt", q)], slot=f"wg{q}")
                    P.add("pool", lambda e: e.dma_start(out=wut[q][:], in_=srcu), w=[("wut", q)], slot=f"wu{q}")
                    return q

                def load_wd(c, grp):
                    q = dcnt[0] % 2; dcnt[0] += 1
                    src = wd[f][grp[0] * 128:(grp[-1] + 1) * 128, c * 128:(c + 1) * 128].rearrange("(j p) n -> p j n", p=128)
                    P.add("pool", lambda e: e.dma_start(out=wdt[q][:, 0:len(grp), :], in_=src), w=[("wdt", q)], slot=f"wd{q}")
                    return q

                seq = [(t, gi) for t in range(NTF) for gi in range(len(groups))]
                pre = load_w(groups[0][0])
                for si, (t, gi) in enumerate(seq):
                    grp = groups[gi]
                    if gi == 0:
                        for s_ in range(TF // NT):
                            norm_tokens(t * TF + s_ * NT, from_x, widx,
                                        (lambda s_=s_: hn[:, :, s_ * NT:(s_ + 1) * NT]), "hn", True)
                    for jj, j in enumerate(grp):
                        q = pre
                        if jj + 1 < len(grp):
                            pre = load_w(grp[jj + 1])
                        elif si + 1 < len(seq):
                            pre = load_w(groups[seq[si + 1][1]][0])
                        for hf in range(NH):
                            b0 = 2 * (pcnt[0] % 2); pcnt[0] += 1
                            pg, pu = psum[b0], psum[b0 + 1]
                            for (wt, pp, nm, bo) in ((wgt, pg, "wgt", 0), (wut, pu, "wut", 1)):
                                for k in range(KC):
                                    P.add("pe", lambda e, wt=wt, pp=pp, k=k, q=q, hf=hf: e.matmul(
                                        pp[:, :], lhsT=wt[q][:, k, :], rhs=hn[:, k, hf * 512:(hf + 1) * 512],
                                        start=(k == 0), stop=(k == KC - 1)),
                                        r=[(nm, q), "hn"], w=[("ps", b0 + bo)])
                            sq_ = scnt[0] % 2; scnt[0] += 1
                            P.add("act", lambda e, pg=pg, sq_=sq_: e.activation(out=sil[sq_][:], in_=pg[:, :], func=AF.Silu),
                                  r=[("ps", b0)], w=[("sil", sq_)])
                            P.add("dve", lambda e, pu=pu, sq_=sq_, jj=jj, hf=hf: e.tensor_tensor(
                                out=hid[:, jj, hf * 512:(hf + 1) * 512], in0=pu[:, :], in1=sil[sq_][:], op=ALU.mult),
                                r=[("ps", b0 + 1), ("sil", sq_)], w=[("hid", jj, hf)])
                    src_old = xT if (from_x and gi == 0) else out
                    dpre = load_wd(0, grp)
                    for c in range(KC):
                        q = dpre
                        if c + 1 < KC:
                            dpre = load_wd(c + 1, grp)
                        for hf in range(NH):
                            pb = 4 + (dps[0] % 2); dps[0] += 1
                            oq = ocnt[0] % 2; ocnt[0] += 1
                            tk0 = t * TF + hf * 512
                            P.add("sp", lambda e, c=c, tk0=tk0, oq=oq, src_old=src_old: e.dma_start(out=oldt[oq][:], in_=src_old[:, c, tk0:tk0 + 512]),
                                  r=[] if src_old is xT else okeys(tk0, 512), w=[("old", oq)], slot=f"old{oq}")
                            for jj in range(len(grp)):
                                P.add("pe", lambda e, pb=pb, q=q, jj=jj, hf=hf, L=len(grp): e.matmul(
                                    psum[pb][:, :], lhsT=wdt[q][:, jj, :], rhs=hid[:, jj, hf * 512:(hf + 1) * 512],
                                    start=(jj == 0), stop=(jj == L - 1)),
                                    r=[("wdt", q), ("hid", jj, hf)], w=[("ps", pb)])
                            P.add("dve", lambda e, pb=pb, oq=oq: e.scalar_tensor_tensor(
                                out=newt[oq][:], in0=psum[pb][:, :], scalar=0.5, in1=oldt[oq][:], op0=ALU.mult, op1=ALU.add),
                                r=[("ps", pb), ("old", oq)], w=[("new", oq)])
                            P.add("sp", lambda e, c=c, tk0=tk0, oq=oq: e.dma_start(out=out[:, c, tk0:tk0 + 512], in_=newt[oq][:]),
                                  r=[("new", oq)], w=okeys(tk0, 512), slot=f"outw{oq}")
                P.barrier()
                P.emit()

        def final_norm_phase():
            with ExitStack() as ns:
                alloc_norm(ns, "fin", 2)
                onc = [ns.enter_context(nc.sbuf_tensor(f"onc{q}", [128, KC, NT], F32)) for q in range(2)]
                for qi, t0 in enumerate(range(0, S, NT)):
                    oq = qi % 2
                    norm_tokens(t0, False, 3, (lambda oq=oq: onc[oq][:]), ("onc", oq), False)
                    P.add("sp", lambda e, t0=t0, oq=oq: e.dma_start(out=out[:, :, t0:t0 + NT], in_=onc[oq][:]), r=[("onc", oq)],
                          w=okeys(t0, NT), slot=f"outf{oq}")
                P.barrier()
                P.emit()

        def mixer_phase():
            with ExitStack() as ms:
                msb = lambda name, shape, dt=F32: ms.enter_context(nc.sbuf_tensor("m_" + name, list(shape), dt))
                SCALE = 128 ** -0.5
                cst = msb("cst", [128, 4, 128]); ident = cst[:, 0, :]; Umat = cst[:, 1, :]; mneg = cst[:, 2, :]; strict = cst[:, 3, :]
                dnw = msb("dnw", [128, 128]); hp = msb("hp", [128, 2, NHD]); convw = msb("convw", [128, 3 * NHD, 4])
                Aexp = msb("Aexp", [128, NHD])
                P.add("sp", lambda e: e.dma_start(out=cst[:], in_=cst_d), w=["cst"], slot="mc0")
                P.add("sp", lambda e: e.dma_start(out=dnw[:], in_=dnw_d), w=["dnw"], slot="mc2")
                P.add("sp", lambda e: e.dma_start(out=hp[:], in_=hp_d), w=["hp"], slot="mc3")
                P.add("sp", lambda e: e.dma_start(out=convw[:], in_=convw_d), w=["convw"], slot="mc4")
                P.add("act", lambda e: e.activation(out=Aexp[:], in_=hp[:, 0, :], func=AF.Exp), r=["hp"], w=["Aexp"])
                with ExitStack() as m0s:
                    alloc_norm(m0s, "m0", 2)
                    hnc = [m0s.enter_context(nc.sbuf_tensor(f"m_hnc{q}", [128, KC, NT], BF16)) for q in range(2)]
                    for qi, t0 in enumerate(range(0, S, NT)):
                        oq = qi % 2
                        norm_tokens(t0, False, 1, (lambda oq=oq: hnc[oq][:]), ("hnc", oq), True)
                        P.add("sp", lambda e, t0=t0, oq=oq: e.dma_start(out=scr_hn[t0 // TT, :, :, (t0 % TT):(t0 % TT) + NT], in_=hnc[oq][:]), r=[("hnc", oq)],
                              w=[("scr_hn", t0 // TT)], slot=f"shn{oq}")
                    P.barrier()
                    P.emit()
                hnt = [msb(f"hnt{q}", [128, KC, TT], BF16) for q in range(2)]
                wch = [msb(f"wch{q}", [128, KC, 128], BF16) for q in range(4)]
                wba = msb("wba", [128, KC, 2 * NHD], BF16)
                dbda = msb("dbda", [128, NTILE, 2 * NHD])
                astk = ExitStack()
                asb = lambda name, shape, dt=F32: astk.enter_context(nc.sbuf_tensor("a_" + name, list(shape), dt))
                QT2 = [asb(f"QT{q}", [128, S], BF16) for q in range(2)]; KT2 = [asb(f"KT{q}", [128, S], BF16) for q in range(2)]
                V2 = [asb(f"V{q}", [128, NTILE, 128], BF16) for q in range(2)]
                Eb = [asb(f"Eb{q}", [128, 512], BF16) for q in range(2)]
                Pb = [asb(f"Pb{q}", [128, 512], BF16) for q in range(2)]
                rec = asb("rec", [128, 512]); attn = asb("attn", [128, 512], BF16)
                amask = asb("amask", [128, cfg.NOFF * 128], BF16)
                ones_b = asb("ones_b", [128, 128], BF16)
                identb = asb("identb", [128, 128], BF16)
                VT2 = [asb(f"VT{q}", [128, S], BF16) for q in range(2)]
                P.add("act", lambda e: e.copy(out=identb[:], in_=ident), r=["cst"], w=["identb"])
                P.add("pool", lambda e: e.dma_start(out=amask[:], in_=amask_d), w=["amask"], slot="mc1")
                P.add("pool", lambda e: e.memset(ones_b[:], 1.0), w=["ones_b"])
                hcnt = [0]

                def load_hn(g):
                    q = hcnt[0] % 2; hcnt[0] += 1
                    P.add("sp", lambda e: e.dma_start(out=hnt[q][:], in_=scr_hn[g]),
                          r=[("scr_hn", g)], w=[("hnt", q)], slot=f"hnt{q}")
                    return q

                def load_wcol(slot_i, col0):
                    src = win[:, col0:col0 + 128].rearrange("(k p) n -> p k n", p=128)
                    P.add("pool", lambda e: e.dma_start(out=wch[slot_i][:], in_=src), w=[("wch", slot_i)], slot=f"wch{slot_i}")

                def fm_proj(slot_i, q, pb, evac):
                    for k in range(KC):
                        P.add("pe", lambda e, k=k: e.matmul(psum[pb][:, 0:TT], lhsT=wch[slot_i][:, k, :], rhs=hnt[q][:, k, :],
                                                            start=(k == 0), stop=(k == KC - 1)),
                              r=[("wch", slot_i), ("hnt", q)], w=[("ps", pb)])
                    evac(pb)

                def tm_proj(w_ap_fn, wkey, q, blk, pb, ncols, evac):
                    for k in range(KC):
                        P.add("pe", lambda e, k=k: e.matmul(psum[pb][:, 0:ncols], lhsT=hnt[q][:, k, blk * 128:(blk + 1) * 128],
                                                            rhs=w_ap_fn(k), start=(k == 0), stop=(k == KC - 1)),
                              r=[wkey, ("hnt", q)], w=[("ps", pb)])
                    evac(pb)

                base_ba = 3 * DA + 4 * DD
                P.add("pool", lambda e: e.dma_start(out=wba[:], in_=win[:, base_ba:base_ba + 2 * NHD].rearrange("(k p) n -> p k n", p=128)),
                      w=["wba"], slot="wba")
                for g in range(NTT):
                    q = load_hn(g)
                    for blk in range(TT // 128):
                        n = g * (TT // 128) + blk
                        tm_proj(lambda k: wba[:, k, :], "wba", q, blk, 6, 2 * NHD,
                                lambda pb, n=n: P.add("act", lambda e: e.copy(out=dbda[:, n, :], in_=psum[pb][:, 0:2 * NHD]),
                                                      r=[("ps", pb)], w=[("dbda", n)]))

                NG = NTT
                NHA_ = NHA if cfg.mixstop >= 1 else 0
                TPG = TT // 128
                for a in range(NHA_):
                    ap_ = a % 2; QT, KT, V, VT = QT2[ap_], KT2[ap_], V2[ap_], VT2[ap_]
                    load_wcol(0, a * 128); load_wcol(1, DA + a * 128); load_wcol(2, 2 * DA + a * 128)
                    for g in range(NTT):
                        q = load_hn(g)
                        fm_proj(0, q, 0, lambda pb, g=g: P.add("act", lambda e, QT=QT: e.copy(out=QT[:, g * TT:(g + 1) * TT], in_=psum[pb][:, 0:TT]),
                                                               r=[("ps", pb)], w=[("QT", ap_, g)]))
                        fm_proj(1, q, 1, lambda pb, g=g: P.add("dve", lambda e, KT=KT: e.tensor_copy(out=KT[:, g * TT:(g + 1) * TT], in_=psum[pb][:, 0:TT]),
                                                               r=[("ps", pb)], w=[("KT", ap_, g)]))
                        fm_proj(2, q, 2, lambda pb, g=g, VT=VT: P.add("act", lambda e: e.copy(out=VT[:, g * TT:(g + 1) * TT], in_=psum[pb][:, 0:TT]),
                                                                      r=[("ps", pb)], w=[("VT", ap_, g)]))
                        for blk in range(TPG):
                            n = g * TPG + blk
                            P.add("pe", lambda e, blk=blk, n=n, VT=VT: e.matmul(psum[3][:, blk * 128:(blk + 1) * 128], lhsT=VT[:, n * 128:(n + 1) * 128],
                                                                                rhs=identb[:], start=True, stop=True),
                                  r=[("VT", ap_, g), "identb"], w=[("ps", 3)])
                        P.add("dve", lambda e, g=g, V=V: e.tensor_copy(out=V[:, g * TPG:(g + 1) * TPG, :].rearrange("p n d -> p (n d)"), in_=psum[3][:, 0:TT]),
                              r=[("ps", 3)], w=[("V", ap_, g * TPG + b_) for b_ in range(TPG)])
                    ec = 0
                    for G in range(NG):
                        mlist = list(range(0, TPG * G + TPG))
                        for mi, m in enumerate(mlist):
                            sb_ = 4 + (ec % 2); eb = ec % 2; ec += 1
                            P.add("pe", lambda e, m=m, G=G, sb_=sb_, KT=KT, QT=QT: e.matmul(psum[sb_][:, 0:TT], lhsT=KT[:, m * 128:(m + 1) * 128],
                                                                              rhs=QT[:, G * TT:(G + 1) * TT], start=True, stop=True),
                                  r=[("KT", ap_, m // TPG), ("QT", ap_, G)], w=[("ps", sb_)])
                            P.add("act", lambda e, sb_=sb_, eb=eb: e.activation(out=Eb[eb][:], in_=psum[sb_][:, 0:TT], func=AF.Exp, scale=SCALE),
                                  r=[("ps", sb_)], w=[("Eb", eb)])
                            o0 = (TPG * G - m + 3) * 128
                            P.add("dve", lambda e, eb=eb, o0=o0: e.tensor_tensor(out=Pb[eb][:], in0=Eb[eb][:], in1=amask[:, o0:o0 + TT], op=ALU.mult),
                                  r=[("Eb", eb), "amask"], w=[("Pb", eb)])
                            P.add("pe", lambda e, m=m, eb=eb, mi=mi, L=len(mlist), V=V: e.matmul(psum[0][:, 0:TT], lhsT=V[:, m, :], rhs=Pb[eb][:],
                                                                                         start=(mi == 0), stop=(mi == L - 1)),
                                  r=[("V", ap_, m), ("Pb", eb)], w=[("ps", 0)])
                            P.add("pe", lambda e, eb=eb, mi=mi, L=len(mlist): e.matmul(psum[1][:, 0:TT], lhsT=ones_b[:], rhs=Pb[eb][:],
                                                                                    start=(mi == 0), stop=(mi == L - 1)),
                                  r=["ones_b", ("Pb", eb)], w=[("ps", 1)])
                        P.add("dve", lambda e: e.reciprocal(out=rec[:], in_=psum[1][:, 0:TT]), r=[("ps", 1)], w=["rec"])
                        P.add("dve", lambda e: e.tensor_tensor(out=attn[:], in0=psum[0][:, 0:TT], in1=rec[:], op=ALU.mult),
                              r=[("ps", 0), "rec"], w=["attn"])
                        P.add("sp", lambda e, a=a, G=G: e.dma_start(out=scr_mix[:, a, G * TT:(G + 1) * TT], in_=attn[:]),
                              r=["attn"], w=[("scr_mix", G)], slot="smx")

                P.barrier()
                P.emit()
                astk.close()
                dstk = ExitStack()
                dsb = lambda name, shape, dt=F32: dstk.enter_context(nc.sbuf_tensor("d_" + name, list(shape), dt))
                pre = [dsb(f"pre{x}", [128, 3 + S]) for x in range(3)]
                cx = [dsb(f"cx{x}", [128, S]) for x in range(3)]
                ZsT = dsb("ZsT", [128, S])
                mixh = dsb("mixh", [128, S], BF16)
                sm = lambda name, shape=(128, 128), dt=F32: dsb(name, list(shape), dt)
                beta = sm("beta", (128, NTILE)); e1 = sm("e1", (128, NTILE)); gg = sm("gg", (128, NTILE))
                gc = sm("gc", (128, NTILE)); ngc = sm("ngc", (128, NTILE)); egc = sm("egc", (128, NTILE)); negc = sm("negc", (128, NTILE))
                gl = sm("gl", (128, NTILE)); egl = sm("egl", (128, NTILE)); ekd = sm("ekd", (128, NTILE))
                sso = sm("sso", (128, 1)); ro = sm("ro", (128, 1))
                Ysb = sm("Ysb"); oA = sm("oA"); vnew = sm("vnew"); osb = sm("osb"); ot = sm("ot"); St = sm("St")
                bsets = []
                for si, banks in enumerate(((2, 3), (0, 1))):
                    B = {"si": si, "banks": banks, "ssq": sm(f"ssq{si}", (128, 2)), "rqk": sm(f"rqk{si}", (128, 2))}
                    for nm in ("tq", "tk", "qn", "kn", "dg", "decT", "Nm", "Mm", "Mt"):
                        B[nm] = sm(f"{nm}{si}")
                    for nm in ("vt", "knT", "qnT", "kdec", "PT", "QKd"):
                        B[nm] = [sm(f"{nm}{si}_{par}") for par in range(2)]
                    bsets.append(B)
                for x in range(3):
                    P.add("pool", lambda e, x=x: e.memset(pre[x][:, 0:3], 0.0), w=[("pre", x)])

                def pq(b, qd):
                    return psum[b][:, qd * 128:(qd + 1) * 128]

                for h in range(NHD if cfg.mixstop >= 2 else 0):
                    cols = (3 * DA + h * 128, 3 * DA + DD + h * 128, 3 * DA + 2 * DD + h * 128, 3 * DA + 3 * DD + h * 128)
                    for x in range(4):
                        load_wcol(x, cols[x])
                    for g in range(NTT):
                        q = load_hn(g)
                        for x in range(3):
                            fm_proj(x, q, (4, 5, 4)[x], lambda pb, x=x, g=g: P.add(
                                "act" if x != 1 else "dve",
                                (lambda e: e.copy(out=pre[x][:, 3 + g * TT:3 + (g + 1) * TT], in_=psum[pb][:, 0:TT])) if x != 1 else
                                (lambda e: e.tensor_copy(out=pre[x][:, 3 + g * TT:3 + (g + 1) * TT], in_=psum[pb][:, 0:TT])),
                                r=[("ps", pb)], w=[("pre", x)]))
                        fm_proj(3, q, 5, lambda pb, g=g: P.add("act", lambda e: e.activation(out=ZsT[:, g * TT:(g + 1) * TT], in_=psum[pb][:, 0:TT], func=AF.Silu),
                                                               r=[("ps", pb)], w=[("ZsT", g)]))
                    dk_ = [("dbda", n) for n in range(NTILE)]
                    P.add("act", lambda e, h=h: e.activation(out=beta[:], in_=dbda[:, :, h], func=AF.Sigmoid), r=dk_, w=["beta"])
                    P.add("act", lambda e, h=h: e.activation(out=e1[:], in_=dbda[:, :, NHD + h], func=AF.Exp, bias=hp[:, 1, h:h + 1]),
                          r=dk_ + ["hp"], w=["e1"])
                    P.add("act", lambda e: e.activation(out=e1[:], in_=e1[:], func=AF.Ln, bias=1.0), r=["e1"], w=["e1"])
                    P.add("dve", lambda e, h=h: e.tensor_scalar(out=gg[:], in0=e1[:], scalar1=Aexp[:, h:h + 1], scalar2=-1.0,
                                                                op0=ALU.mult, op1=ALU.mult), r=["e1", "Aexp"], w=["gg"])
                    P.add("pe", lambda e: e.matmul(pq(7, 0)[:, 0:NTILE], lhsT=Umat, rhs=gg[:], start=True, stop=True),
                          r=["cst", "gg"], w=[("ps", 7)])
                    P.add("pe", lambda e: e.matmul(pq(7, 1)[:, 0:NTILE], lhsT=ones_f[:], rhs=gg[:], start=True, stop=True),
                          r=["ones_f", "gg"], w=[("ps", 7)])
                    P.add("dve", lambda e: e.tensor_copy(out=gc[:], in_=pq(7, 0)[:, 0:NTILE]), r=[("ps", 7)], w=["gc"])
                    P.add("dve", lambda e: e.tensor_copy(out=gl[:], in_=pq(7, 1)[:, 0:NTILE]), r=[("ps", 7)], w=["gl"])
                    P.add("dve", lambda e: e.tensor_scalar_mul(out=ngc[:], in0=gc[:], scalar1=-1.0), r=["gc"], w=["ngc"])
                    P.add("act", lambda e: e.activation(out=egc[:], in_=gc[:], func=AF.Exp), r=["gc"], w=["egc"])
                    P.add("dve", lambda e: e.tensor_scalar_mul(out=negc[:], in0=egc[:], scalar1=-1.0), r=["egc"], w=["negc"])
                    P.add("act", lambda e: e.activation(out=egl[:], in_=gl[:], func=AF.Exp), r=["gl"], w=["egl"])
                    P.add("dve", lambda e: e.tensor_tensor(out=ekd[:], in0=gl[:], in1=gc[:], op=ALU.subtract), r=["gl", "gc"], w=["ekd"])
                    P.add("act", lambda e: e.activation(out=ekd[:], in_=ekd[:], func=AF.Exp), r=["ekd"], w=["ekd"])
                    for x in range(3):
                        ch = x * NHD + h
                        P.add("dve", lambda e, x=x, ch=ch: e.tensor_scalar_mul(out=cx[x][:], in0=pre[x][:, 0:S], scalar1=convw[:, ch, 0:1]),
                              r=[("pre", x), "convw"], w=[("cx", x)])
                        for i in range(1, 4):
                            P.add("dve", lambda e, x=x, ch=ch, i=i: e.scalar_tensor_tensor(
                                out=cx[x][:], in0=pre[x][:, i:i + S], scalar=convw[:, ch, i:i + 1], in1=cx[x][:], op0=ALU.mult, op1=ALU.add),
                                r=[("pre", x), "convw", ("cx", x)], w=[("cx", x)])
                        P.add("act", lambda e, x=x: e.activation(out=cx[x][:], in_=cx[x][:], func=AF.Silu), r=[("cx", x)], w=[("cx", x)])
                    P.add("pool", lambda e: e.memset(St[:], 0.0), w=["St"])
                    OUTS = ("vt", "knT", "qnT", "kdec", "PT", "QKd")

                    def prep(n, B, par):
                        si = B["si"]; b0, b1 = B["banks"]
                        K_ = lambda nm: (nm, si, par) if nm in OUTS else (nm, si)
                        ts = slice(n * 128, (n + 1) * 128); nn = slice(n, n + 1)
                        tq, tk, ssq, rqk, qn, kn, dg, decT, Nm, Mm, Mt = (B[k] for k in (
                            "tq", "tk", "ssq", "rqk", "qn", "kn", "dg", "decT", "Nm", "Mm", "Mt"))
                        vt, knT, qnT, kdec, PT, QKd = (B[k][par] for k in OUTS)
                        for x in range(3):
                            P.add("pe", lambda e, x=x: e.matmul(pq(b0, x), lhsT=cx[x][:, ts], rhs=ident, start=True, stop=True),
                                  r=[("cx", x), "cst"], w=[("ps", b0)])
                        yield
                        P.add("act", lambda e: e.activation(out=tq[:], in_=pq(b0, 0), func=AF.Square), r=[("ps", b0)], w=[K_("tq")]); yield
                        P.add("act", lambda e: e.activation(out=tk[:], in_=pq(b0, 1), func=AF.Square), r=[("ps", b0)], w=[K_("tk")]); yield
                        P.add("act", lambda e: e.copy(out=vt[:], in_=pq(b0, 2)), r=[("ps", b0)], w=[K_("vt")]); yield
                        P.add("dve", lambda e: e.reduce_sum(out=ssq[:, 0:1], in_=tq[:], axis=mybir.AxisListType.X), r=[K_("tq")], w=[K_("ssq")]); yield
                        P.add("dve", lambda e: e.reduce_sum(out=ssq[:, 1:2], in_=tk[:], axis=mybir.AxisListType.X), r=[K_("tk"), K_("ssq")], w=[K_("ssq")]); yield
                        P.add("act", lambda e: e.activation(out=rqk[:], in_=ssq[:], func=AF.Sqrt, bias=EPS), r=[K_("ssq")], w=[K_("rqk")]); yield
                        P.add("dve", lambda e: e.reciprocal(out=rqk[:], in_=rqk[:]), r=[K_("rqk")], w=[K_("rqk")]); yield
                        P.add("dve", lambda e: e.tensor_scalar(out=qn[:], in0=pq(b0, 0), scalar1=rqk[:, 0:1], scalar2=SCALE,
                                                               op0=ALU.mult, op1=ALU.mult), r=[("ps", b0), K_("rqk")], w=[K_("qn")]); yield
                        P.add("dve", lambda e: e.tensor_scalar_mul(out=kn[:], in0=pq(b0, 1), scalar1=rqk[:, 1:2]), r=[("ps", b0), K_("rqk")], w=[K_("kn")]); yield
                        P.add("pe", lambda e: e.matmul(pq(b1, 0), lhsT=kn[:], rhs=ident, start=True, stop=True), r=[K_("kn"), "cst"], w=[("ps", b1)])
                        P.add("pe", lambda e: e.matmul(pq(b1, 1), lhsT=qn[:], rhs=ident, start=True, stop=True), r=[K_("qn"), "cst"], w=[("ps", b1)]); yield
                        P.add("dve", lambda e: e.tensor_copy(out=knT[:], in_=pq(b1, 0)), r=[("ps", b1)], w=[K_("knT")]); yield
                        P.add("dve", lambda e: e.tensor_copy(out=qnT[:], in_=pq(b1, 1)), r=[("ps", b1)], w=[K_("qnT")]); yield
                        P.add("dve", lambda e: e.tensor_scalar_mul(out=kdec[:], in0=kn[:], scalar1=ekd[:, nn]), r=[K_("kn"), "ekd"], w=[K_("kdec")]); yield
                        P.add("dve", lambda e: e.tensor_scalar_mul(out=dg[:], in0=ident, scalar1=gc[:, nn]), r=["cst", "gc"], w=[K_("dg")]); yield
                        P.add("pe", lambda e: e.matmul(pq(b0, 3), lhsT=ones_f[:], rhs=dg[:], start=True, stop=False), r=["ones_f", K_("dg")], w=[("ps", b0)])
                        P.add("pe", lambda e: e.matmul(pq(b0, 3), lhsT=ident, rhs=mneg, start=False, stop=True), r=["cst"], w=[("ps", b0)])
                        P.add("pe", lambda e: e.matmul(pq(b1, 2), lhsT=knT[:], rhs=knT[:], start=True, stop=True), r=[K_("knT")], w=[("ps", b1)])
                        P.add("pe", lambda e: e.matmul(pq(b1, 3), lhsT=knT[:], rhs=qnT[:], start=True, stop=True), r=[K_("knT"), K_("qnT")], w=[("ps", b1)]); yield
                        P.add("act", lambda e: e.activation(out=decT[:], in_=pq(b0, 3), func=AF.Exp, bias=ngc[:, nn]), r=[("ps", b0), "ngc"], w=[K_("decT")]); yield
                        P.add("dve", lambda e: e.scalar_tensor_tensor(out=Nm[:], in0=pq(b1, 2), scalar=beta[:, nn], in1=decT[:],
                                                                      op0=ALU.mult, op1=ALU.mult), r=[("ps", b1), "beta", K_("decT")], w=[K_("Nm")]); yield
                        P.add("dve", lambda e: e.tensor_tensor(out=QKd[:], in0=pq(b1, 3), in1=decT[:], op=ALU.mult), r=[("ps", b1), K_("decT")], w=[K_("QKd")]); yield
                        P.add("dve", lambda e: e.tensor_tensor(out=Nm[:], in0=Nm[:], in1=strict, op=ALU.mult), r=[K_("Nm"), "cst"], w=[K_("Nm")]); yield
                        P.add("pe", lambda e: e.matmul(pq(b1, 0), lhsT=Nm[:], rhs=ident, start=True, stop=True), r=[K_("Nm"), "cst"], w=[("ps", b1)]); yield
                        P.add("dve", lambda e: e.tensor_copy(out=Mt[:], in_=pq(b1, 0)), r=[("ps", b1)], w=[K_("Mt")]); yield
                        P.add("dve", lambda e: e.tensor_tensor(out=PT[:], in0=ident, in1=Nm[:], op=ALU.subtract), r=["cst", K_("Nm")], w=[K_("PT")]); yield
                        P.add("act", lambda e: e.copy(out=Mm[:], in_=Nm[:]), r=[K_("Nm")], w=[K_("Mm")]); yield
                        for st in range(6):
                            if st < 5:
                                P.add("pe", lambda e: e.matmul(pq(b0, 0), lhsT=Mt[:], rhs=Mm[:], start=True, stop=True), r=[K_("Mt"), K_("Mm")], w=[("ps", b0)])
                            P.add("pe", lambda e: e.matmul(pq(b1, 1), lhsT=Mm[:], rhs=Mt[:], start=True, stop=True), r=[K_("Mt"), K_("Mm")], w=[("ps", b1)]); yield
                            if st < 5:
                                P.add("act", lambda e: e.copy(out=Mm[:], in_=pq(b0, 0)), r=[("ps", b0)], w=[K_("Mm")]); yield
                            P.add("dve", lambda e: e.tensor_copy(out=Mt[:], in_=pq(b1, 1)), r=[("ps", b1)], w=[K_("Mt")]); yield
                            P.add("pe", lambda e: e.matmul(pq(b1, 2), lhsT=Mt[:], rhs=PT[:], start=True, stop=True), r=[K_("Mt"), K_("PT")], w=[("ps", b1)]); yield
                            P.add("dve", lambda e: e.tensor_tensor(out=PT[:], in0=PT[:], in1=pq(b1, 2), op=ALU.add), r=[K_("PT"), ("ps", b1)], w=[K_("PT")]); yield

                    def rec(n, B, par):
                        si = B["si"]
                        K_ = lambda nm: (nm, si, par)
                        ts = slice(n * 128, (n + 1) * 128); nn = slice(n, n + 1)
                        vt, knT, qnT, kdec, PT, QKd = (B[k][par] for k in OUTS)
                        P.add("pe", lambda e: e.matmul(pq(7, 0), lhsT=knT[:], rhs=St[:], start=True, stop=True), r=[K_("knT"), "St"], w=[("ps", 7)])
                        yield
                        P.add("pe", lambda e: e.matmul(pq(7, 1), lhsT=qnT[:], rhs=St[:], start=True, stop=True), r=[K_("qnT"), "St"], w=[("ps", 7)])
                        yield
                        P.add("dve", lambda e: e.scalar_tensor_tensor(out=Ysb[:], in0=pq(7, 0), scalar=negc[:, nn], in1=vt[:],
                                                                      op0=ALU.mult, op1=ALU.add), r=[("ps", 7), "negc", K_("vt")], w=["Ysb"])
                        yield
                        P.add("dve", lambda e: e.tensor_scalar_mul(out=oA[:], in0=pq(7, 1), scalar1=egc[:, nn]), r=[("ps", 7), "egc"], w=["oA"])
                        yield
                        P.add("pe", lambda e: e.matmul(pq(7, 2), lhsT=PT[:], rhs=Ysb[:], start=True, stop=True), r=[K_("PT"), "Ysb"], w=[("ps", 7)])
                        yield
                        P.add("dve", lambda e: e.tensor_scalar_mul(out=vnew[:], in0=pq(7, 2), scalar1=beta[:, nn]), r=[("ps", 7), "beta"], w=["vnew"])
                        yield
                        P.add("pe", lambda e: e.matmul(pq(7, 3), lhsT=QKd[:], rhs=vnew[:], start=True, stop=True), r=[K_("QKd"), "vnew"], w=[("ps", 7)])
                        yield
                        P.add("pe", lambda e: e.matmul(pq(7, 0), lhsT=kdec[:], rhs=vnew[:], start=True, stop=True), r=[K_("kdec"), "vnew"], w=[("ps", 7)])
                        yield
                        P.add("dve", lambda e: e.tensor_tensor(out=osb[:], in0=pq(7, 3), in1=oA[:], op=ALU.add), r=[("ps", 7), "oA"], w=["osb"])
                        yield
                        P.add("dve", lambda e: e.scalar_tensor_tensor(out=St[:], in0=St[:], scalar=egl[:, nn], in1=pq(7, 0),
                                                                      op0=ALU.mult, op1=ALU.add), r=["St", "egl", ("ps", 7)], w=["St"])
                        yield
                        P.add("act", lambda e: e.activation(out=ot[:], in_=osb[:], func=AF.Square), r=["osb"], w=["ot"])
                        yield
                        P.add("dve", lambda e: e.reduce_sum(out=sso[:], in_=ot[:], axis=mybir.AxisListType.X), r=["ot"], w=["sso"])
                        yield
                        P.add("act", lambda e: e.activation(out=ro[:], in_=sso[:], func=AF.Sqrt, bias=EPS, scale=1.0 / 128), r=["sso"], w=["ro"])
                        yield
                        P.add("dve", lambda e: e.reciprocal(out=ro[:], in_=ro[:]), r=["ro"], w=["ro"])
                        yield
                        P.add("dve", lambda e: e.scalar_tensor_tensor(out=ot[:], in0=osb[:], scalar=ro[:, 0:1], in1=dnw[:], op0=ALU.mult, op1=ALU.mult),
                              r=["osb", "ro", "dnw"], w=["ot"])
                        yield
                        P.add("pe", lambda e: e.matmul(pq(6, 0), lhsT=ot[:], rhs=ident, start=True, stop=True), r=["ot", "cst"], w=[("ps", 6)])
                        yield
                        P.add("dve", lambda e: e.tensor_tensor(out=mixh[:, ts], in0=pq(6, 0), in1=ZsT[:, ts], op=ALU.mult),
                              r=[("ps", 6), ("ZsT", n // TPG)], w=["mixh"])
                        yield

                    def rec_pair(n0, par):
                        for d in range(2):
                            if n0 + d < NTILE:
                                yield from rec(n0 + d, bsets[d], par)

                    def run_rr(gens):
                        live = list(gens)
                        while live:
                            for g_ in list(live):
                                try:
                                    next(g_)
                                except StopIteration:
                                    live.remove(g_)

                    pending = None
                    for pi_, n0 in enumerate(range(0, NTILE, 2)):
                        par = pi_ % 2
                        gens = [prep(n0 + d, bsets[d], par) for d in range(2) if n0 + d < NTILE]
                        if pending is not None:
                            gens.append(pending)
                        run_rr(gens)
                        pending = rec_pair(n0, par)
                    run_rr([pending])
                    P.add("sp", lambda e, h=h: e.dma_start(out=scr_mix[:, NHA + h, :], in_=mixh[:]), r=["mixh"],
                          w=[("scr_mix", G) for G in range(NTT)], slot="smx2")

                P.barrier()
                P.emit()
                dstk.close()
                mxt = list(hnt) + [msb(f"mxt{q}", [128, KC, TT], BF16) for q in range(max(0, NTT - 2))]
                wot = [wch[0], wch[1]]
                oldm = [msb(f"oldm{q}", [128, TT]) for q in range(2)]
                newm = [msb(f"newm{q}", [128, TT]) for q in range(2)]
                ngs = NTT if cfg.mixstop >= 3 else 0
                for g in range(ngs):
                    P.add("sp", lambda e, g=g: e.dma_start(out=mxt[g][:, 0:KM, :], in_=scr_mix[:, :, g * TT:(g + 1) * TT]),
                          r=[("scr_mix", g)], w=[("hnt", g % 2) if g < 2 else ("mxt", g)], slot=f"mxl{g % 2}")
                wc = 0; oc = 0
                for c in range(KC if ngs else 0):
                    wq = wc % 2; wc += 1
                    src = wo[:, c * 128:(c + 1) * 128].rearrange("(k p) n -> p k n", p=128)
                    P.add("pool", lambda e, wq=wq, src=src: e.dma_start(out=wot[wq][:, 0:KM, :], in_=src), w=[("wch", wq)], slot=f"wch{wq}")
                    for g in range(ngs):
                        oq = oc % 2; oc += 1
                        P.add("sp", lambda e, oq=oq, c=c, g=g: e.dma_start(out=oldm[oq][:], in_=out[:, c, g * TT:(g + 1) * TT]),
                              r=okeys(g * TT, TT), w=[("oldm", oq)], slot=f"oldm{oq}")
                        pb = 4 + (oc % 2)
                        for k in range(KM):
                            P.add("pe", lambda e, pb=pb, wq=wq, k=k, g=g: e.matmul(psum[pb][:, 0:TT], lhsT=wot[wq][:, k, :], rhs=mxt[g][:, k, :],
                                                                                  start=(k == 0), stop=(k == KM - 1)),
                                  r=[("wch", wq), ("hnt", g % 2) if g < 2 else ("mxt", g)], w=[("ps", pb)])
                        P.add("dve", lambda e, pb=pb, oq=oq: e.tensor_tensor(out=newm[oq][:], in0=psum[pb][:, 0:TT], in1=oldm[oq][:], op=ALU.add),
                              r=[("ps", pb), ("oldm", oq)], w=[("newm", oq)])
                        P.add("sp", lambda e, oq=oq, c=c, g=g: e.dma_start(out=out[:, c, g * TT:(g + 1) * TT], in_=newm[oq][:]),
                              r=[("newm", oq)], w=okeys(g * TT, TT), slot=f"outm{oq}")
                P.barrier()
                P.emit()

        from_x = True
        for ph in phases:
            if ph == "ffn1":
                ffn_phase(0, 0, from_x)
            elif ph == "ffn2":
                ffn_phase(1, 2, from_x)
            elif ph == "mix":
                if from_x:
                    raise ValueError("mixer phase needs the residual in `out` (run an FFN phase first)")
                mixer_phase()
            from_x = False
        final_norm_phase()
        P.barrier()
        P.emit()
    return nc


def _fm(a):
    t, d = a.shape
    return np.ascontiguousarray(a.reshape(t, d // 128, 128).transpose(2, 1, 0))


def _amask(cfg):
    def mult(dl):
        return ((dl >= 0) & (dl <= 128)).astype(np.float32) + ((dl >= 0) & (dl <= 512) & (dl % 4 == 0)) + \
               ((dl >= 0) & (dl <= 2048) & (dl % 16 == 0))
    k = np.arange(128)[:, None]
    qq = np.arange(128)[None, :]
    tiles = [mult(off * 128 + qq - k) for off in range(-3, cfg.NTILE)]
    return np.ascontiguousarray(np.concatenate(tiles, axis=1), dtype=np.float32)


def _cst():
    j = np.arange(128)[:, None]
    i = np.arange(128)[None, :]
    ident = (i == j).astype(np.float32)
    U = (j <= i).astype(np.float32)
    mneg = np.where(i >= j, 0.0, -30000.0).astype(np.float32)
    strict = (i > j).astype(np.float32)
    return np.ascontiguousarray(np.stack([ident, U, mneg, strict], axis=1))


def prep_inputs(cfg, inputs):
    D, KC, B, NHD = cfg.D, cfg.KC, cfg.B, cfg.NHD
    f32 = lambda a: np.ascontiguousarray(np.asarray(a), dtype=np.float32)
    x = f32(inputs["x"])
    nrm = np.ascontiguousarray(np.stack([f32(inputs[k]).reshape(KC, 128).T for k in
                                         ("ffn1_norm", "mix_norm", "ffn2_norm", "final_norm")], axis=1))
    shared = {"norms": nrm}
    for f, pre in ((1, "ffn1"), (2, "ffn2")):
        shared[f"wg{f}"] = f32(inputs[pre + "_w_gate"]); shared[f"wu{f}"] = f32(inputs[pre + "_w_up"])
        shared[f"wd{f}"] = f32(inputs[pre + "_w_down"])
    shared["win"] = f32(inputs["w_in"]); shared["wo"] = f32(inputs["w_out"])
    cw = f32(inputs["conv_w"])
    shared["convw"] = np.ascontiguousarray(cw.reshape(4, 3 * NHD, 128).transpose(2, 1, 0))
    shared["hp"] = np.ascontiguousarray(np.broadcast_to(np.stack([f32(inputs["a_log"]), f32(inputs["dt_bias"])])[None], (128, 2, NHD)))
    shared["dnw"] = np.ascontiguousarray(np.broadcast_to(f32(inputs["dn_norm"])[None, :], (128, 128)))
    shared["cst"] = _cst(); shared["amask"] = _amask(cfg)
    maps = []
    for b in range(B):
        m = dict(shared)
        m["xT"] = _fm(x[b])
        maps.append(m)
    return maps


def assemble(cfg, results):
    out = np.empty((cfg.B, cfg.S, cfg.D), np.float32)
    for b in range(cfg.B):
        o = np.asarray(results[b]["out"]).reshape(128, cfg.KC, cfg.S)
        out[b] = o.transpose(2, 1, 0).reshape(cfg.S, cfg.D)
    return out


def kernel(**inputs):
    cfg = Cfg()
    nc = build(cfg)
    maps = prep_inputs(cfg, inputs)
    res = run_bass_kernel_spmd(nc, maps, core_ids=list(range(cfg.B)))
    return assemble(cfg, res.results)
```

```python
import numpy as np
import ml_dtypes
import concourse.bass as bass
import concourse.mybir as mybir
from concourse.bass_utils import run_bass_kernel_spmd

F32 = mybir.dt.float32
BF16 = mybir.dt.bfloat16
ALU = mybir.AluOpType
AF = mybir.ActivationFunctionType
NCORE = 8
EPS = 1e-6


class Op:
    __slots__ = ("eng", "fn", "deps", "kind", "slot", "val", "sig", "ord", "idx")

    def __init__(self, eng, fn, kind):
        self.eng = eng; self.fn = fn; self.kind = kind
        self.deps = []; self.slot = None; self.val = 0; self.sig = False; self.ord = 0


class Prog:
    ENGS = ("pe", "act", "dve", "pool", "sp")

    def __init__(self, nc, stack):
        self.nc = nc
        self.stack = stack
        self.pending = {e: [] for e in self.ENGS}
        self.last_w = {}
        self.readers = {}
        self.slot_cnt = {}
        self.slot_sem = {}
        self.eng_sem = {e: stack.enter_context(nc.semaphore("es_" + e)) for e in ("pe", "act", "dve", "pool")}
        self.eng_cnt = {e: 0 for e in self.ENGS}
        self.waited = {e: {} for e in self.ENGS}
        self.last_op = {e: None for e in self.ENGS}
        self.open_dmas = []

    def _slot(self, slot):
        if slot not in self.slot_sem:
            self.slot_sem[slot] = self.stack.enter_context(self.nc.semaphore("ds_" + slot))
            self.slot_cnt[slot] = 0
        return self.slot_sem[slot]

    def add(self, eng, fn, r=(), w=(), slot=None, cc=False, extra=()):
        kind = "cc" if cc else ("dma" if slot is not None else "cmp")
        op = Op(eng, fn, kind)
        deps = []
        for k in r:
            lw = self.last_w.get(k)
            if lw is not None:
                deps.append(lw)
            if isinstance(k, tuple) and k[0] == "ps":
                deps.extend(rd for rd in self.readers.get(k, ()) if rd.eng != eng)
        for k in w:
            lw = self.last_w.get(k)
            if lw is not None:
                deps.append(lw)
            deps.extend(self.readers.get(k, ()))
        deps.extend(extra)
        for k in r:
            self.readers.setdefault(k, []).append(op)
        for k in w:
            self.last_w[k] = op
            self.readers[k] = []
        seen = set()
        for d in deps:
            if d is op or id(d) in seen:
                continue
            seen.add(id(d))
            if d.kind == "cmp":
                if d.eng == "pe" and eng == "pe":
                    continue
                d.sig = True
            op.deps.append(d)
        if slot is not None:
            self._slot(slot)
            self.slot_cnt[slot] += 1
            op.slot = slot
            op.val = self.slot_cnt[slot] * (1 if cc else 16)
            self.open_dmas.append(op)
        self.pending[eng].append(op)
        self.last_op[eng] = op
        return op

    def barrier(self):
        lasts = [o for o in self.last_op.values() if o is not None and o.kind == "cmp"]
        dmas = list({o.slot: o for o in self.open_dmas}.values())
        self.open_dmas = []
        for e in self.ENGS:
            self.add(e, None, extra=lasts + dmas)

    def emit(self):
        nc = self.nc
        engs = {"pe": "tensor", "act": "scalar", "dve": "vector", "pool": "gpsimd", "sp": "sync"}
        pend = self.pending
        self.pending = {e: [] for e in self.ENGS}
        for e in self.ENGS:
            for op in pend[e]:
                if op.kind == "cmp" and op.sig:
                    self.eng_cnt[e] += 1
                    op.ord = self.eng_cnt[e]
        with nc.Block() as block:
            for e in self.ENGS:
                ops = pend[e]
                if not ops:
                    continue

                def section(engine, ops=ops, e=e):
                    waited = self.waited[e]
                    for op in ops:
                        need = {}
                        for d in op.deps:
                            if d.kind == "cmp":
                                key = ("e", d.eng); v = d.ord
                            else:
                                key = ("s", d.slot); v = d.val
                            if v > need.get(key, 0):
                                need[key] = v
                        for key, v in need.items():
                            if waited.get(key, 0) >= v:
                                continue
                            sem = self.eng_sem[key[1]] if key[0] == "e" else self.slot_sem[key[1]]
                            engine.wait_ge(sem, v)
                            waited[key] = v
                        if op.fn is None:
                            continue
                        ins = op.fn(engine)
                        if op.kind == "cmp":
                            if op.sig:
                                ins.then_inc(self.eng_sem[e], 1)
                        elif op.kind == "dma":
                            ins.then_inc(self.slot_sem[op.slot], 16)
                        else:
                            ins.then_inc(self.slot_sem[op.slot], 1)

                getattr(block, engs[e])(section)


class Cfg:
    def __init__(self, D=4096, FF=11008, S=2048, B=4, NHA=16, NHD=16):
        self.D = D; self.FF = FF; self.S = S; self.B = B; self.NHA = NHA; self.NHD = NHD
        self.KC = D // 128
        self.NJ = FF // 128
        self.DA = NHA * 128; self.DD = NHD * 128
        self.KM = NHA + NHD
        self.WIN = 3 * self.DA + 4 * self.DD + 2 * NHD
        self.TT = 512
        self.TF = min(1024, S)
        self.NGRP = 4
        self.NTT = S // self.TT
        self.NTILE = S // 128
        self.NT = 64
        self.NOFF = self.NTILE + 3
        self.mixstop = 9


def build(cfg, phases=("ffn1", "mix", "ffn2")):
    from contextlib import ExitStack
    D, KC, FF, NJ, S, TT, NTT, NT, NTILE = cfg.D, cfg.KC, cfg.FF, cfg.NJ, cfg.S, cfg.TT, cfg.NTT, cfg.NT, cfg.NTILE
    NHA, NHD, DA, DD, KM, WIN = cfg.NHA, cfg.NHD, cfg.DA, cfg.DD, cfg.KM, cfg.WIN
    nc = bass.Bass("TRN2", target_bir_lowering=False)
    dt_in = lambda name, shape: nc.dram_tensor(name, list(shape), F32, kind="ExternalInput").ap()
    xT = dt_in("xT", [128, KC, S])
    norms = dt_in("norms", [128, 4, KC])
    wg = [dt_in("wg1", [D, FF]), dt_in("wg2", [D, FF])]
    wu = [dt_in("wu1", [D, FF]), dt_in("wu2", [D, FF])]
    wd = [dt_in("wd1", [FF, D]), dt_in("wd2", [FF, D])]
    win = dt_in("win", [D, WIN])
    convw_d = dt_in("convw", [128, 3 * NHD, 4])
    hp_d = dt_in("hp", [128, 2, NHD])
    dnw_d = dt_in("dnw", [128, 128])
    wo = dt_in("wo", [KM * 128, D])
    cst_d = dt_in("cst", [128, 4, 128])
    amask_d = dt_in("amask", [128, cfg.NOFF * 128])
    out = nc.dram_tensor("out", [128, KC, S], F32, kind="ExternalOutput").ap()
    scr_hn = nc.dram_tensor("scr_hn", [128, KC, S], BF16, kind="ExternalOutput").ap()
    scr_mix = nc.dram_tensor("scr_mix", [128, KM, S], BF16, kind="ExternalOutput").ap()

    with ExitStack() as stack:
        P = Prog(nc, stack)
        sb = lambda name, shape, dt=F32: stack.enter_context(nc.sbuf_tensor(name, list(shape), dt))
        ones_f = sb("ones_f", [128, 128])
        normw = sb("normw", [128, 4, KC])
        psum = [stack.enter_context(nc.psum_tensor(f"ps{i}", [128, 512], F32)) for i in range(8)]
        P.add("pool", lambda e: e.memset(ones_f[:], 1.0), w=["ones_f"])
        P.add("sp", lambda e: e.dma_start(out=normw[:], in_=norms), w=["normw"], slot="const")
        NB = {}

        def alloc_norm(stk, tag, nbuf):
            a = lambda name, shape: stk.enter_context(nc.sbuf_tensor(f"{tag}_{name}", list(shape), F32))
            NB.clear()
            NB.update(n=nbuf, cnt=0, banks=(6, 7) if nbuf > 1 else (6,),
                      hc=[a(f"hc{q}", [128, KC, NT]) for q in range(nbuf)], sq=[a(f"sq{q}", [128, KC, NT]) for q in range(nbuf)],
                      rstd=[a(f"rstd{q}", [128, NT]) for q in range(nbuf)], ssum=[a(f"ssum{q}", [128, NT]) for q in range(nbuf)])

        def okeys(t0, n):
            return [("out", q) for q in range(t0 // NT, (t0 + n + NT - 1) // NT)]

        def norm_tokens(t0, from_x, widx, dst_fn, dst_key, bf):
            srcr = xT if from_x else out
            i = NB["cnt"] % NB["n"]; NB["cnt"] += 1
            hc, sq, rstd, ssum = NB["hc"][i], NB["sq"][i], NB["rstd"][i], NB["ssum"][i]
            pb = NB["banks"][i % len(NB["banks"])]
            hk, sk, rk, mk = ("hc", i), ("sq", i), ("rstd", i), ("ssum", i)
            P.add("sp", lambda e: e.dma_start(out=hc[:], in_=srcr[:, :, t0:t0 + NT]),
                  r=[] if from_x else okeys(t0, NT), w=[hk], slot=f"hc{i}")
            P.add("act", lambda e: e.activation(out=sq[:], in_=hc[:], func=AF.Square), r=[hk], w=[sk])
            P.add("dve", lambda e: e.reduce_sum(out=ssum[:], in_=sq[:].rearrange("p c t -> p t c"), axis=mybir.AxisListType.X),
                  r=[sk], w=[mk])
            P.add("pe", lambda e: e.matmul(psum[pb][:, 0:NT], lhsT=ones_f[:], rhs=ssum[:], start=True, stop=True),
                  r=[mk, "ones_f"], w=[("ps", pb)])
            P.add("act", lambda e: e.activation(out=rstd[:], in_=psum[pb][:, 0:NT], func=AF.Sqrt, bias=EPS, scale=1.0 / D),
                  r=[("ps", pb)], w=[rk])
            P.add("dve", lambda e: e.reciprocal(out=rstd[:], in_=rstd[:]), r=[rk], w=[rk])
            P.add("dve", lambda e: e.tensor_tensor(out=sq[:], in0=hc[:],
                                                   in1=rstd[:].unsqueeze(1).broadcast_to([128, KC, NT]), op=ALU.mult),
                  r=[hk, rk], w=[sk])
            P.add("dve", lambda e: e.tensor_tensor(out=dst_fn(), in0=sq[:],
                                                   in1=normw[:, widx, :].unsqueeze(2).broadcast_to([128, KC, NT]),
                                                   op=ALU.mult), r=[sk, "normw"], w=[dst_key])

        def ffn_phase(f, widx, from_x):
            with ExitStack() as fs:
                fsb = lambda name, shape, dt=F32: fs.enter_context(nc.sbuf_tensor(f"f{f}_" + name, list(shape), dt))
                TF = cfg.TF; NH = TF // 512; NTF = S // TF
                alloc_norm(fs, f"f{f}n", 2)
                groups = [list(g) for g in np.array_split(np.arange(NJ), cfg.NGRP) if len(g)]
                GM = max(len(g) for g in groups)
                hn = fsb("hn", [128, KC, TF], BF16)
                wgt = [fsb(f"wgt{q}", [128, KC, 128], BF16) for q in range(2)]
                wut = [fsb(f"wut{q}", [128, KC, 128], BF16) for q in range(2)]
                hid = fsb("hid", [128, GM, TF], BF16)
                wdt = [fsb(f"wdt{q}", [128, GM, 128], BF16) for q in range(2)]
                sil = [fsb(f"sil{q}", [128, 512]) for q in range(2)]
                oldt = [fsb(f"old{q}", [128, 512]) for q in range(2)]
                newt = [fsb(f"new{q}", [128, 512]) for q in range(2)]
                wcnt = [0]; dcnt = [0]; pcnt = [0]; scnt = [0]; ocnt = [0]; dps = [0]

                def load_w(j):
                    q = wcnt[0] % 2; wcnt[0] += 1
                    srcg = wg[f][:, j * 128:(j + 1) * 128].rearrange("(k p) n -> p k n", p=128)
                    srcu = wu[f][:, j * 128:(j + 1) * 128].rearrange("(k p) n -> p k n", p=128)
                    P.add("pool", lambda e: e.dma_start(out=wgt[q][:], in_=srcg), w=[("wgt", q)], slot=f"wg{q}")
                    P.add("pool", lambda e: e.dma_start(out=wut[q][:], in_=srcu), w=[("wut", q)], slot=f"wu{q}")
                    return q

                def load_wd(c, grp):
                    q = dcnt[0] % 2; dcnt[0] += 1
                    src = wd[f][grp[0] * 128:(grp[-1] + 1) * 128, c * 128:(c + 1) * 128].rearrange("(j p) n -> p j n", p=128)
                    P.add("pool", lambda e: e.dma_start(out=wdt[q][:, 0:len(grp), :], in_=src), w=[("wdt", q)], slot=f"wd{q}")
                    return q

                seq = [(t, gi) for t in range(NTF) for gi in range(len(groups))]
                pre = load_w(groups[0][0])
                for si, (t, gi) in enumerate(seq):
                    grp = groups[gi]
                    if gi == 0:
                        for s_ in range(TF // NT):
                            norm_tokens(t * TF + s_ * NT, from_x, widx,
                                        (lambda s_=s_: hn[:, :, s_ * NT:(s_ + 1) * NT]), "hn", True)
                    for jj, j in enumerate(grp):
                        q = pre
                        if jj + 1 < len(grp):
                            pre = load_w(grp[jj + 1])
                        elif si + 1 < len(seq):
                            pre = load_w(groups[seq[si + 1][1]][0])
                        for hf in range(NH):
                            b0 = 2 * (pcnt[0] % 2); pcnt[0] += 1
                            pg, pu = psum[b0], psum[b0 + 1]
                            for (wt, pp, nm, bo) in ((wgt, pg, "wgt", 0), (wut, pu, "wut", 1)):
                                for k in range(KC):
                                    P.add("pe", lambda e, wt=wt, pp=pp, k=k, q=q, hf=hf: e.matmul(
                                        pp[:, :], lhsT=wt[q][:, k, :], rhs=hn[:, k, hf * 512:(hf + 1) * 512],
                                        start=(k == 0), stop=(k == KC - 1)),
                                        r=[(nm, q), "hn"], w=[("ps", b0 + bo)])
                            sq_ = scnt[0] % 2; scnt[0] += 1
                            P.add("act", lambda e, pg=pg, sq_=sq_: e.activation(out=sil[sq_][:], in_=pg[:, :], func=AF.Silu),
                                  r=[("ps", b0)], w=[("sil", sq_)])
                            P.add("dve", lambda e, pu=pu, sq_=sq_, jj=jj, hf=hf: e.tensor_tensor(
                                out=hid[:, jj, hf * 512:(hf + 1) * 512], in0=pu[:, :], in1=sil[sq_][:], op=ALU.mult),
                                r=[("ps", b0 + 1), ("sil", sq_)], w=[("hid", jj, hf)])
                    src_old = xT if (from_x and gi == 0) else out
                    dpre = load_wd(0, grp)
                    for c in range(KC):
                        q = dpre
                        if c + 1 < KC:
                            dpre = load_wd(c + 1, grp)
                        for hf in range(NH):
                            pb = 4 + (dps[0] % 2); dps[0] += 1
                            oq = ocnt[0] % 2; ocnt[0] += 1
                            tk0 = t * TF + hf * 512
                            P.add("sp", lambda e, c=c, tk0=tk0, oq=oq, src_old=src_old: e.dma_start(out=oldt[oq][:], in_=src_old[:, c, tk0:tk0 + 512]),
                                  r=[] if src_old is xT else okeys(tk0, 512), w=[("old", oq)], slot=f"old{oq}")
                            for jj in range(len(grp)):
                                P.add("pe", lambda e, pb=pb, q=q, jj=jj, hf=hf, L=len(grp): e.matmul(
                                    psum[pb][:, :], lhsT=wdt[q][:, jj, :], rhs=hid[:, jj, hf * 512:(hf + 1) * 512],
                                    start=(jj == 0), stop=(jj == L - 1)),
                                    r=[("wdt", q), ("hid", jj, hf)], w=[("ps", pb)])
                            P.add("dve", lambda e, pb=pb, oq=oq: e.scalar_tensor_tensor(
                                out=newt[oq][:], in0=psum[pb][:, :], scalar=0.5, in1=oldt[oq][:], op0=ALU.mult, op1=ALU.add),
                                r=[("ps", pb), ("old", oq)], w=[("new", oq)])
                            P.add("sp", lambda e, c=c, tk0=tk0, oq=oq: e.dma_start(out=out[:, c, tk0:tk0 + 512], in_=newt[oq][:]),
                                  r=[("new", oq)], w=okeys(tk0, 512), slot=f"outw{oq}")
                P.barrier()
                P.emit()

        def final_norm_phase():
            with ExitStack() as ns:
                alloc_norm(ns, "fin", 2)
                onc = [ns.enter_context(nc.sbuf_tensor(f"onc{q}", [128, KC, NT], F32)) for q in range(2)]
                for qi, t0 in enumerate(range(0, S, NT)):
                    oq = qi % 2
                    norm_tokens(t0, False, 3, (lambda oq=oq: onc[oq][:]), ("onc", oq), False)
                    P.add("sp", lambda e, t0=t0, oq=oq: e.dma_start(out=out[:, :, t0:t0 + NT], in_=onc[oq][:]), r=[("onc", oq)],
                          w=okeys(t0, NT), slot=f"outf{oq}")
                P.barrier()
                P.emit()

        def mixer_phase():
            with ExitStack() as ms:
                msb = lambda name, shape, dt=F32: ms.enter_context(nc.sbuf_tensor("m_" + name, list(shape), dt))
                SCALE = 128 ** -0.5
                cst = msb("cst", [128, 4, 128]); ident = cst[:, 0, :]; Umat = cst[:, 1, :]; mneg = cst[:, 2, :]; strict = cst[:, 3, :]
                dnw = msb("dnw", [128, 128]); hp = msb("hp", [128, 2, NHD]); convw = msb("convw", [128, 3 * NHD, 4])
                Aexp = msb("Aexp", [128, NHD])
                P.add("sp", lambda e: e.dma_start(out=cst[:], in_=cst_d), w=["cst"], slot="mc0")
                P.add("sp", lambda e: e.dma_start(out=dnw[:], in_=dnw_d), w=["dnw"], slot="mc2")
                P.add("sp", lambda e: e.dma_start(out=hp[:], in_=hp_d), w=["hp"], slot="mc3")
                P.add("sp", lambda e: e.dma_start(out=convw[:], in_=convw_d), w=["convw"], slot="mc4")
                P.add("act", lambda e: e.activation(out=Aexp[:], in_=hp[:, 0, :], func=AF.Exp), r=["hp"], w=["Aexp"])
                with ExitStack() as m0s:
                    alloc_norm(m0s, "m0", 2)
                    hnc = [m0s.enter_context(nc.sbuf_tensor(f"m_hnc{q}", [128, KC, NT], BF16)) for q in range(2)]
                    for qi, t0 in enumerate(range(0, S, NT)):
                        oq = qi % 2
                        norm_tokens(t0, False, 1, (lambda oq=oq: hnc[oq][:]), ("hnc", oq), True)
                        P.add("sp", lambda e, t0=t0, oq=oq: e.dma_start(out=scr_hn[:, :, t0:t0 + NT], in_=hnc[oq][:]), r=[("hnc", oq)],
                              w=[("scr_hn", t0 // TT)], slot=f"shn{oq}")
                    P.barrier()
                    P.emit()
                hnt = [msb(f"hnt{q}", [128, KC, TT], BF16) for q in range(2)]
                wch = [msb(f"wch{q}", [128, KC, 128], BF16) for q in range(4)]
                wba = msb("wba", [128, KC, 2 * NHD], BF16)
                dbda = msb("dbda", [128, NTILE, 2 * NHD])
                astk = ExitStack()
                asb = lambda name, shape, dt=F32: astk.enter_context(nc.sbuf_tensor("a_" + name, list(shape), dt))
                QT2 = [asb(f"QT{q}", [128, S], BF16) for q in range(2)]; KT2 = [asb(f"KT{q}", [128, S], BF16) for q in range(2)]
                V2 = [asb(f"V{q}", [128, NTILE, 128], BF16) for q in range(2)]
                Eb = [asb(f"Eb{q}", [128, 512], BF16) for q in range(2)]
                Pb = [asb(f"Pb{q}", [128, 512], BF16) for q in range(2)]
                rec = asb("rec", [128, 512]); attn = asb("attn", [128, 512], BF16)
                amask = asb("amask", [128, cfg.NOFF * 128], BF16)
                ones_b = asb("ones_b", [128, 128], BF16)
                identb = asb("identb", [128, 128], BF16)
                VT2 = [asb(f"VT{q}", [128, S], BF16) for q in range(2)]
                P.add("act", lambda e: e.copy(out=identb[:], in_=ident), r=["cst"], w=["identb"])
                P.add("pool", lambda e: e.dma_start(out=amask[:], in_=amask_d), w=["amask"], slot="mc1")
                P.add("pool", lambda e: e.memset(ones_b[:], 1.0), w=["ones_b"])
                hcnt = [0]

                def load_hn(g):
                    q = hcnt[0] % 2; hcnt[0] += 1
                    P.add("sp", lambda e: e.dma_start(out=hnt[q][:], in_=scr_hn[:, :, g * TT:(g + 1) * TT]),
                          r=[("scr_hn", g)], w=[("hnt", q)], slot=f"hnt{q}")
                    return q

                def load_wcol(slot_i, col0):
                    src = win[:, col0:col0 + 128].rearrange("(k p) n -> p k n", p=128)
                    P.add("pool", lambda e: e.dma_start(out=wch[slot_i][:], in_=src), w=[("wch", slot_i)], slot=f"wch{slot_i}")

                def fm_proj(slot_i, q, pb, evac):
                    for k in range(KC):
                        P.add("pe", lambda e, k=k: e.matmul(psum[pb][:, 0:TT], lhsT=wch[slot_i][:, k, :], rhs=hnt[q][:, k, :],
                                                            start=(k == 0), stop=(k == KC - 1)),
                              r=[("wch", slot_i), ("hnt", q)], w=[("ps", pb)])
                    evac(pb)

                def tm_proj(w_ap_fn, wkey, q, blk, pb, ncols, evac):
                    for k in range(KC):
                        P.add("pe", lambda e, k=k: e.matmul(psum[pb][:, 0:ncols], lhsT=hnt[q][:, k, blk * 128:(blk + 1) * 128],
                                                            rhs=w_ap_fn(k), start=(k == 0), stop=(k == KC - 1)),
                              r=[wkey, ("hnt", q)], w=[("ps", pb)])
                    evac(pb)

                base_ba = 3 * DA + 4 * DD
                P.add("pool", lambda e: e.dma_start(out=wba[:], in_=win[:, base_ba:base_ba + 2 * NHD].rearrange("(k p) n -> p k n", p=128)),
                      w=["wba"], slot="wba")
                for g in range(NTT):
                    q = load_hn(g)
                    for blk in range(TT // 128):
                        n = g * (TT // 128) + blk
                        tm_proj(lambda k: wba[:, k, :], "wba", q, blk, 6, 2 * NHD,
                                lambda pb, n=n: P.add("act", lambda e: e.copy(out=dbda[:, n, :], in_=psum[pb][:, 0:2 * NHD]),
                                                      r=[("ps", pb)], w=[("dbda", n)]))

                NG = NTT
                NHA_ = NHA if cfg.mixstop >= 1 else 0
                TPG = TT // 128
                for a in range(NHA_):
                    ap_ = a % 2; QT, KT, V, VT = QT2[ap_], KT2[ap_], V2[ap_], VT2[ap_]
                    load_wcol(0, a * 128); load_wcol(1, DA + a * 128); load_wcol(2, 2 * DA + a * 128)
                    for g in range(NTT):
                        q = load_hn(g)
                        fm_proj(0, q, 0, lambda pb, g=g: P.add("act", lambda e, QT=QT: e.copy(out=QT[:, g * TT:(g + 1) * TT], in_=psum[pb][:, 0:TT]),
                                                               r=[("ps", pb)], w=[("QT", ap_, g)]))
                        fm_proj(1, q, 1, lambda pb, g=g: P.add("dve", lambda e, KT=KT: e.tensor_copy(out=KT[:, g * TT:(g + 1) * TT], in_=psum[pb][:, 0:TT]),
                                                               r=[("ps", pb)], w=[("KT", ap_, g)]))
                        fm_proj(2, q, 2, lambda pb, g=g, VT=VT: P.add("act", lambda e: e.copy(out=VT[:, g * TT:(g + 1) * TT], in_=psum[pb][:, 0:TT]),
                                                                      r=[("ps", pb)], w=[("VT", ap_, g)]))
                        for blk in range(TPG):
                            n = g * TPG + blk
                            P.add("pe", lambda e, blk=blk, n=n, VT=VT: e.matmul(psum[3][:, blk * 128:(blk + 1) * 128], lhsT=VT[:, n * 128:(n + 1) * 128],
                                                                                rhs=identb[:], start=True, stop=True),
                                  r=[("VT", ap_, g), "identb"], w=[("ps", 3)])
                        P.add("dve", lambda e, g=g, V=V: e.tensor_copy(out=V[:, g * TPG:(g + 1) * TPG, :].rearrange("p n d -> p (n d)"), in_=psum[3][:, 0:TT]),
                              r=[("ps", 3)], w=[("V", ap_, g * TPG + b_) for b_ in range(TPG)])
                    pairs = [(G, mi, m, TPG * G + TPG) for G in range(NG) for mi, m in enumerate(range(0, TPG * G + TPG))]

                    def emit_st(idx):
                        G, mi, m, L = pairs[idx]
                        sb_ = 4 + (idx % 2)
                        P.add("pe", lambda e, m=m, G=G, sb_=sb_, KT=KT, QT=QT: e.matmul(psum[sb_][:, 0:TT], lhsT=KT[:, m * 128:(m + 1) * 128],
                                                                                          rhs=QT[:, G * TT:(G + 1) * TT], start=True, stop=True),
                              r=[("KT", ap_, m // TPG), ("QT", ap_, G)], w=[("ps", sb_)])

                    if pairs:
                        emit_st(0)
                    for idx, (G, mi, m, L) in enumerate(pairs):
                        sb_ = 4 + (idx % 2); eb = idx % 2
                        if idx + 1 < len(pairs):
                            emit_st(idx + 1)
                        P.add("act", lambda e, sb_=sb_, eb=eb: e.activation(out=Eb[eb][:], in_=psum[sb_][:, 0:TT], func=AF.Exp, scale=SCALE),
                              r=[("ps", sb_)], w=[("Eb", eb)])
                        o0 = (TPG * G - m + 3) * 128
                        P.add("dve", lambda e, eb=eb, o0=o0: e.tensor_tensor(out=Pb[eb][:], in0=Eb[eb][:], in1=amask[:, o0:o0 + TT], op=ALU.mult),
                              r=[("Eb", eb), "amask"], w=[("Pb", eb)])
                        P.add("pe", lambda e, m=m, eb=eb, mi=mi, L=L, V=V: e.matmul(psum[0][:, 0:TT], lhsT=V[:, m, :], rhs=Pb[eb][:],
                                                                                  start=(mi == 0), stop=(mi == L - 1)),
                              r=[("V", ap_, m), ("Pb", eb)], w=[("ps", 0)])
                        P.add("pe", lambda e, eb=eb, mi=mi, L=L: e.matmul(psum[1][:, 0:TT], lhsT=ones_b[:], rhs=Pb[eb][:],
                                                                        start=(mi == 0), stop=(mi == L - 1)),
                              r=["ones_b", ("Pb", eb)], w=[("ps", 1)])
                        if mi == L - 1:
                            P.add("dve", lambda e: e.reciprocal(out=rec[:], in_=psum[1][:, 0:TT]), r=[("ps", 1)], w=["rec"])
                            P.add("dve", lambda e: e.tensor_tensor(out=attn[:], in0=psum[0][:, 0:TT], in1=rec[:], op=ALU.mult),
                                  r=[("ps", 0), "rec"], w=["attn"])
                            P.add("sp", lambda e, a=a, G=G: e.dma_start(out=scr_mix[:, a, G * TT:(G + 1) * TT], in_=attn[:]),
                                  r=["attn"], w=[("scr_mix", G)], slot="smx")

                P.barrier()
                P.emit()
                astk.close()
                dstk = ExitStack()
                dsb = lambda name, shape, dt=F32: dstk.enter_context(nc.sbuf_tensor("d_" + name, list(shape), dt))
                pre = [dsb(f"pre{x}", [128, 3 + S]) for x in range(3)]
                cx = [dsb(f"cx{x}", [128, S]) for x in range(3)]
                ZsT = dsb("ZsT", [128, S])
                mixh = dsb("mixh", [128, S], BF16)
                sm = lambda name, shape=(128, 128), dt=F32: dsb(name, list(shape), dt)
                beta = sm("beta", (128, NTILE)); e1 = sm("e1", (128, NTILE)); gg = sm("gg", (128, NTILE))
                gc = sm("gc", (128, NTILE)); ngc = sm("ngc", (128, NTILE)); egc = sm("egc", (128, NTILE)); negc = sm("negc", (128, NTILE))
                gl = sm("gl", (128, NTILE)); egl = sm("egl", (128, NTILE)); ekd = sm("ekd", (128, NTILE))
                sso = sm("sso", (128, 1)); ro = sm("ro", (128, 1))
                Ysb = sm("Ysb"); oA = sm("oA"); vnew = sm("vnew"); osb = sm("osb"); ot = sm("ot"); St = sm("St")
                bsets = []
                for si, banks in enumerate(((2, 3), (0, 1))):
                    B = {"si": si, "banks": banks, "ssq": sm(f"ssq{si}", (128, 2)), "rqk": sm(f"rqk{si}", (128, 2))}
                    for nm in ("tq", "tk", "qn", "kn", "dg", "decT", "Nm", "Mm", "Mt"):
                        B[nm] = sm(f"{nm}{si}")
                    for nm in ("vt", "knT", "qnT", "kdec", "PT", "QKd"):
                        B[nm] = [sm(f"{nm}{si}_{par}") for par in range(2)]
                    bsets.append(B)
                for x in range(3):
                    P.add("pool", lambda e, x=x: e.memset(pre[x][:, 0:3], 0.0), w=[("pre", x)])

                def pq(b, qd):
                    return psum[b][:, qd * 128:(qd + 1) * 128]

                for h in range(NHD if cfg.mixstop >= 2 else 0):
                    cols = (3 * DA + h * 128, 3 * DA + DD + h * 128, 3 * DA + 2 * DD + h * 128, 3 * DA + 3 * DD + h * 128)
                    for x in range(4):
                        load_wcol(x, cols[x])
                    for g in range(NTT):
                        q = load_hn(g)
                        for x in range(3):
                            fm_proj(x, q, (4, 5, 4)[x], lambda pb, x=x, g=g: P.add(
                                "act" if x != 1 else "dve",
                                (lambda e: e.copy(out=pre[x][:, 3 + g * TT:3 + (g + 1) * TT], in_=psum[pb][:, 0:TT])) if x != 1 else
                                (lambda e: e.tensor_copy(out=pre[x][:, 3 + g * TT:3 + (g + 1) * TT], in_=psum[pb][:, 0:TT])),
                                r=[("ps", pb)], w=[("pre", x)]))
                        fm_proj(3, q, 5, lambda pb, g=g: P.add("act", lambda e: e.activation(out=ZsT[:, g * TT:(g + 1) * TT], in_=psum[pb][:, 0:TT], func=AF.Silu),
                                                               r=[("ps", pb)], w=[("ZsT", g)]))
                    dk_ = [("dbda", n) for n in range(NTILE)]
                    P.add("act", lambda e, h=h: e.activation(out=beta[:], in_=dbda[:, :, h], func=AF.Sigmoid), r=dk_, w=["beta"])
                    P.add("act", lambda e, h=h: e.activation(out=e1[:], in_=dbda[:, :, NHD + h], func=AF.Exp, bias=hp[:, 1, h:h + 1]),
                          r=dk_ + ["hp"], w=["e1"])
                    P.add("act", lambda e: e.activation(out=e1[:], in_=e1[:], func=AF.Ln, bias=1.0), r=["e1"], w=["e1"])
                    P.add("dve", lambda e, h=h: e.tensor_scalar(out=gg[:], in0=e1[:], scalar1=Aexp[:, h:h + 1], scalar2=-1.0,
                                                                op0=ALU.mult, op1=ALU.mult), r=["e1", "Aexp"], w=["gg"])
                    P.add("pe", lambda e: e.matmul(pq(7, 0)[:, 0:NTILE], lhsT=Umat, rhs=gg[:], start=True, stop=True),
                          r=["cst", "gg"], w=[("ps", 7)])
                    P.add("pe", lambda e: e.matmul(pq(7, 1)[:, 0:NTILE], lhsT=ones_f[:], rhs=gg[:], start=True, stop=True),
                          r=["ones_f", "gg"], w=[("ps", 7)])
                    P.add("dve", lambda e: e.tensor_copy(out=gc[:], in_=pq(7, 0)[:, 0:NTILE]), r=[("ps", 7)], w=["gc"])
                    P.add("dve", lambda e: e.tensor_copy(out=gl[:], in_=pq(7, 1)[:, 0:NTILE]), r=[("ps", 7)], w=["gl"])
                    P.add("dve", lambda e: e.tensor_scalar_mul(out=ngc[:], in0=gc[:], scalar1=-1.0), r=["gc"], w=["ngc"])
                    P.add("act", lambda e: e.activation(out=egc[:], in_=gc[:], func=AF.Exp), r=["gc"], w=["egc"])
                    P.add("dve", lambda e: e.tensor_scalar_mul(out=negc[:], in0=egc[:], scalar1=-1.0), r=["egc"], w=["negc"])
                    P.add("act", lambda e: e.activation(out=egl[:], in_=gl[:], func=AF.Exp), r=["gl"], w=["egl"])
                    P.add("dve", lambda e: e.tensor_tensor(out=ekd[:], in0=gl[:], in1=gc[:], op=ALU.subtract), r=["gl", "gc"], w=["ekd"])
                    P.add("act", lambda e: e.activation(out=ekd[:], in_=ekd[:], func=AF.Exp), r=["ekd"], w=["ekd"])
                    for x in range(3):
                        ch = x * NHD + h
                        P.add("dve", lambda e, x=x, ch=ch: e.tensor_scalar_mul(out=cx[x][:], in0=pre[x][:, 0:S], scalar1=convw[:, ch, 0:1]),
                              r=[("pre", x), "convw"], w=[("cx", x)])
                        for i in range(1, 4):
                            P.add("dve", lambda e, x=x, ch=ch, i=i: e.scalar_tensor_tensor(
                                out=cx[x][:], in0=pre[x][:, i:i + S], scalar=convw[:, ch, i:i + 1], in1=cx[x][:], op0=ALU.mult, op1=ALU.add),
                                r=[("pre", x), "convw", ("cx", x)], w=[("cx", x)])
                        P.add("act", lambda e, x=x: e.activation(out=cx[x][:], in_=cx[x][:], func=AF.Silu), r=[("cx", x)], w=[("cx", x)])
                    P.add("pool", lambda e: e.memset(St[:], 0.0), w=["St"])
                    OUTS = ("vt", "knT", "qnT", "kdec", "PT", "QKd")

                    def prep(n, B, par):
                        si = B["si"]; b0, b1 = B["banks"]
                        K_ = lambda nm: (nm, si, par) if nm in OUTS else (nm, si)
                        ts = slice(n * 128, (n + 1) * 128); nn = slice(n, n + 1)
                        tq, tk, ssq, rqk, qn, kn, dg, decT, Nm, Mm, Mt = (B[k] for k in (
                            "tq", "tk", "ssq", "rqk", "qn", "kn", "dg", "decT", "Nm", "Mm", "Mt"))
                        vt, knT, qnT, kdec, PT, QKd = (B[k][par] for k in OUTS)
                        for x in range(3):
                            P.add("pe", lambda e, x=x: e.matmul(pq(b0, x), lhsT=cx[x][:, ts], rhs=ident, start=True, stop=True),
                                  r=[("cx", x), "cst"], w=[("ps", b0)])
                        yield
                        P.add("act", lambda e: e.activation(out=tq[:], in_=pq(b0, 0), func=AF.Square), r=[("ps", b0)], w=[K_("tq")]); yield
                        P.add("act", lambda e: e.activation(out=tk[:], in_=pq(b0, 1), func=AF.Square), r=[("ps", b0)], w=[K_("tk")]); yield
                        P.add("act", lambda e: e.copy(out=vt[:], in_=pq(b0, 2)), r=[("ps", b0)], w=[K_("vt")]); yield
                        P.add("dve", lambda e: e.reduce_sum(out=ssq[:, 0:1], in_=tq[:], axis=mybir.AxisListType.X), r=[K_("tq")], w=[K_("ssq")]); yield
                        P.add("dve", lambda e: e.reduce_sum(out=ssq[:, 1:2], in_=tk[:], axis=mybir.AxisListType.X), r=[K_("tk"), K_("ssq")], w=[K_("ssq")]); yield
                        P.add("act", lambda e: e.activation(out=rqk[:], in_=ssq[:], func=AF.Sqrt, bias=EPS), r=[K_("ssq")], w=[K_("rqk")]); yield
                        P.add("dve", lambda e: e.reciprocal(out=rqk[:], in_=rqk[:]), r=[K_("rqk")], w=[K_("rqk")]); yield
                        P.add("dve", lambda e: e.tensor_scalar(out=qn[:], in0=pq(b0, 0), scalar1=rqk[:, 0:1], scalar2=SCALE,
                                                               op0=ALU.mult, op1=ALU.mult), r=[("ps", b0), K_("rqk")], w=[K_("qn")]); yield
                        P.add("dve", lambda e: e.tensor_scalar_mul(out=kn[:], in0=pq(b0, 1), scalar1=rqk[:, 1:2]), r=[("ps", b0), K_("rqk")], w=[K_("kn")]); yield
                        P.add("pe", lambda e: e.matmul(pq(b1, 0), lhsT=kn[:], rhs=ident, start=True, stop=True), r=[K_("kn"), "cst"], w=[("ps", b1)])
                        P.add("pe", lambda e: e.matmul(pq(b1, 1), lhsT=qn[:], rhs=ident, start=True, stop=True), r=[K_("qn"), "cst"], w=[("ps", b1)]); yield
                        P.add("dve", lambda e: e.tensor_copy(out=knT[:], in_=pq(b1, 0)), r=[("ps", b1)], w=[K_("knT")]); yield
                        P.add("dve", lambda e: e.tensor_copy(out=qnT[:], in_=pq(b1, 1)), r=[("ps", b1)], w=[K_("qnT")]); yield
                        P.add("dve", lambda e: e.tensor_scalar_mul(out=kdec[:], in0=kn[:], scalar1=ekd[:, nn]), r=[K_("kn"), "ekd"], w=[K_("kdec")]); yield
                        P.add("dve", lambda e: e.tensor_scalar_mul(out=dg[:], in0=ident, scalar1=gc[:, nn]), r=["cst", "gc"], w=[K_("dg")]); yield
                        P.add("pe", lambda e: e.matmul(pq(b0, 3), lhsT=ones_f[:], rhs=dg[:], start=True, stop=False), r=["ones_f", K_("dg")], w=[("ps", b0)])
                        P.add("pe", lambda e: e.matmul(pq(b0, 3), lhsT=ident, rhs=mneg, start=False, stop=True), r=["cst"], w=[("ps", b0)])
                        P.add("pe", lambda e: e.matmul(pq(b1, 2), lhsT=knT[:], rhs=knT[:], start=True, stop=True), r=[K_("knT")], w=[("ps", b1)])
                        P.add("pe", lambda e: e.matmul(pq(b1, 3), lhsT=knT[:], rhs=qnT[:], start=True, stop=True), r=[K_("knT"), K_("qnT")], w=[("ps", b1)]); yield
                        P.add("act", lambda e: e.activation(out=decT[:], in_=pq(b0, 3), func=AF.Exp, bias=ngc[:, nn]), r=[("ps", b0), "ngc"], w=[K_("decT")]); yield
                        P.add("dve", lambda e: e.scalar_tensor_tensor(out=Nm[:], in0=pq(b1, 2), scalar=beta[:, nn], in1=decT[:],
                                                                      op0=ALU.mult, op1=ALU.mult), r=[("ps", b1), "beta", K_("decT")], w=[K_("Nm")]); yield
                        P.add("dve", lambda e: e.tensor_tensor(out=QKd[:], in0=pq(b1, 3), in1=decT[:], op=ALU.mult), r=[("ps", b1), K_("decT")], w=[K_("QKd")]); yield
                        P.add("dve", lambda e: e.tensor_tensor(out=Nm[:], in0=Nm[:], in1=strict, op=ALU.mult), r=[K_("Nm"), "cst"], w=[K_("Nm")]); yield
                        P.add("pe", lambda e: e.matmul(pq(b1, 0), lhsT=Nm[:], rhs=ident, start=True, stop=True), r=[K_("Nm"), "cst"], w=[("ps", b1)]); yield
                        P.add("dve", lambda e: e.tensor_copy(out=Mt[:], in_=pq(b1, 0)), r=[("ps", b1)], w=[K_("Mt")]); yield
                        P.add("dve", lambda e: e.tensor_tensor(out=PT[:], in0=ident, in1=Nm[:], op=ALU.subtract), r=["cst", K_("Nm")], w=[K_("PT")]); yield
                        P.add("act", lambda e: e.copy(out=Mm[:], in_=Nm[:]), r=[K_("Nm")], w=[K_("Mm")]); yield
                        for st in range(6):
                            if st < 5:
                                P.add("pe", lambda e: e.matmul(pq(b0, 0), lhsT=Mt[:], rhs=Mm[:], start=True, stop=True), r=[K_("Mt"), K_("Mm")], w=[("ps", b0)])
                            P.add("pe", lambda e: e.matmul(pq(b1, 1), lhsT=Mm[:], rhs=Mt[:], start=True, stop=True), r=[K_("Mt"), K_("Mm")], w=[("ps", b1)]); yield
                            if st < 5:
                                P.add("act", lambda e: e.copy(out=Mm[:], in_=pq(b0, 0)), r=[("ps", b0)], w=[K_("Mm")]); yield
                            P.add("dve", lambda e: e.tensor_copy(out=Mt[:], in_=pq(b1, 1)), r=[("ps", b1)], w=[K_("Mt")]); yield
                            P.add("pe", lambda e: e.matmul(pq(b1, 2), lhsT=Mt[:], rhs=PT[:], start=True, stop=True), r=[K_("Mt"), K_("PT")], w=[("ps", b1)]); yield
                            P.add("dve", lambda e: e.tensor_tensor(out=PT[:], in0=PT[:], in1=pq(b1, 2), op=ALU.add), r=[K_("PT"), ("ps", b1)], w=[K_("PT")]); yield

                    def rec(n, B, par):
                        si = B["si"]
                        K_ = lambda nm: (nm, si, par)
                        ts = slice(n * 128, (n + 1) * 128); nn = slice(n, n + 1)
                        vt, knT, qnT, kdec, PT, QKd = (B[k][par] for k in OUTS)
                        P.add("pe", lambda e: e.matmul(pq(7, 0), lhsT=knT[:], rhs=St[:], start=True, stop=True), r=[K_("knT"), "St"], w=[("ps", 7)])
                        yield
                        P.add("pe", lambda e: e.matmul(pq(7, 1), lhsT=qnT[:], rhs=St[:], start=True, stop=True), r=[K_("qnT"), "St"], w=[("ps", 7)])
                        yield
                        P.add("dve", lambda e: e.scalar_tensor_tensor(out=Ysb[:], in0=pq(7, 0), scalar=negc[:, nn], in1=vt[:],
                                                                      op0=ALU.mult, op1=ALU.add), r=[("ps", 7), "negc", K_("vt")], w=["Ysb"])
                        yield
                        P.add("dve", lambda e: e.tensor_scalar_mul(out=oA[:], in0=pq(7, 1), scalar1=egc[:, nn]), r=[("ps", 7), "egc"], w=["oA"])
                        yield
                        P.add("pe", lambda e: e.matmul(pq(7, 2), lhsT=PT[:], rhs=Ysb[:], start=True, stop=True), r=[K_("PT"), "Ysb"], w=[("ps", 7)])
                        yield
                        P.add("dve", lambda e: e.tensor_scalar_mul(out=vnew[:], in0=pq(7, 2), scalar1=beta[:, nn]), r=[("ps", 7), "beta"], w=["vnew"])
                        yield
                        P.add("pe", lambda e: e.matmul(pq(7, 3), lhsT=QKd[:], rhs=vnew[:], start=True, stop=True), r=[K_("QKd"), "vnew"], w=[("ps", 7)])
                        yield
                        P.add("pe", lambda e: e.matmul(pq(7, 0), lhsT=kdec[:], rhs=vnew[:], start=True, stop=True), r=[K_("kdec"), "vnew"], w=[("ps", 7)])
                        yield
                        P.add("dve", lambda e: e.tensor_tensor(out=osb[:], in0=pq(7, 3), in1=oA[:], op=ALU.add), r=[("ps", 7), "oA"], w=["osb"])
                        yield
                        P.add("dve", lambda e: e.scalar_tensor_tensor(out=St[:], in0=St[:], scalar=egl[:, nn], in1=pq(7, 0),
                                                                      op0=ALU.mult, op1=ALU.add), r=["St", "egl", ("ps", 7)], w=["St"])
                        yield
                        P.add("act", lambda e: e.activation(out=ot[:], in_=osb[:], func=AF.Square), r=["osb"], w=["ot"])
                        yield
                        P.add("dve", lambda e: e.reduce_sum(out=sso[:], in_=ot[:], axis=mybir.AxisListType.X), r=["ot"], w=["sso"])
                        yield
                        P.add("act", lambda e: e.activation(out=ro[:], in_=sso[:], func=AF.Sqrt, bias=EPS, scale=1.0 / 128), r=["sso"], w=["ro"])
                        yield
                        P.add("dve", lambda e: e.reciprocal(out=ro[:], in_=ro[:]), r=["ro"], w=["ro"])
                        yield
                        P.add("dve", lambda e: e.scalar_tensor_tensor(out=ot[:], in0=osb[:], scalar=ro[:, 0:1], in1=dnw[:], op0=ALU.mult, op1=ALU.mult),
                              r=["osb", "ro", "dnw"], w=["ot"])
                        yield
                        P.add("pe", lambda e: e.matmul(pq(6, 0), lhsT=ot[:], rhs=ident, start=True, stop=True), r=["ot", "cst"], w=[("ps", 6)])
                        yield
                        P.add("dve", lambda e: e.tensor_tensor(out=mixh[:, ts], in0=pq(6, 0), in1=ZsT[:, ts], op=ALU.mult),
                              r=[("ps", 6), ("ZsT", n // TPG)], w=["mixh"])
                        yield

                    def rec_pair(n0, par):
                        for d in range(2):
                            if n0 + d < NTILE:
                                yield from rec(n0 + d, bsets[d], par)

                    def run_rr(gens):
                        live = list(gens)
                        while live:
                            for g_ in list(live):
                                try:
                                    next(g_)
                                except StopIteration:
                                    live.remove(g_)

                    pending = None
                    for pi_, n0 in enumerate(range(0, NTILE, 2)):
                        par = pi_ % 2
                        gens = [prep(n0 + d, bsets[d], par) for d in range(2) if n0 + d < NTILE]
                        if pending is not None:
                            gens.append(pending)
                        run_rr(gens)
                        pending = rec_pair(n0, par)
                    run_rr([pending])
                    P.add("sp", lambda e, h=h: e.dma_start(out=scr_mix[:, NHA + h, :], in_=mixh[:]), r=["mixh"],
                          w=[("scr_mix", G) for G in range(NTT)], slot="smx2")

                P.barrier()
                P.emit()
                dstk.close()
                mxt = list(hnt) + [msb(f"mxt{q}", [128, KC, TT], BF16) for q in range(max(0, NTT - 2))]
                wot = [wch[0], wch[1]]
                oldm = [msb(f"oldm{q}", [128, TT]) for q in range(2)]
                newm = [msb(f"newm{q}", [128, TT]) for q in range(2)]
                ngs = NTT if cfg.mixstop >= 3 else 0
                for g in range(ngs):
                    P.add("sp", lambda e, g=g: e.dma_start(out=mxt[g][:, 0:KM, :], in_=scr_mix[:, :, g * TT:(g + 1) * TT]),
                          r=[("scr_mix", g)], w=[("hnt", g % 2) if g < 2 else ("mxt", g)], slot=f"mxl{g % 2}")
                wc = 0; oc = 0
                for c in range(KC if ngs else 0):
                    wq = wc % 2; wc += 1
                    src = wo[:, c * 128:(c + 1) * 128].rearrange("(k p) n -> p k n", p=128)
                    P.add("pool", lambda e, wq=wq, src=src: e.dma_start(out=wot[wq][:, 0:KM, :], in_=src), w=[("wch", wq)], slot=f"wch{wq}")
                    for g in range(ngs):
                        oq = oc % 2; oc += 1
                        P.add("sp", lambda e, oq=oq, c=c, g=g: e.dma_start(out=oldm[oq][:], in_=out[:, c, g * TT:(g + 1) * TT]),
                              r=okeys(g * TT, TT), w=[("oldm", oq)], slot=f"oldm{oq}")
                        pb = 4 + (oc % 2)
                        for k in range(KM):
                            P.add("pe", lambda e, pb=pb, wq=wq, k=k, g=g: e.matmul(psum[pb][:, 0:TT], lhsT=wot[wq][:, k, :], rhs=mxt[g][:, k, :],
                                                                                  start=(k == 0), stop=(k == KM - 1)),
                                  r=[("wch", wq), ("hnt", g % 2) if g < 2 else ("mxt", g)], w=[("ps", pb)])
                        P.add("dve", lambda e, pb=pb, oq=oq: e.tensor_tensor(out=newm[oq][:], in0=psum[pb][:, 0:TT], in1=oldm[oq][:], op=ALU.add),
                              r=[("ps", pb), ("oldm", oq)], w=[("newm", oq)])
                        P.add("sp", lambda e, oq=oq, c=c, g=g: e.dma_start(out=out[:, c, g * TT:(g + 1) * TT], in_=newm[oq][:]),
                              r=[("newm", oq)], w=okeys(g * TT, TT), slot=f"outm{oq}")
                P.barrier()
                P.emit()

        from_x = True
        for ph in phases:
            if ph == "ffn1":
                ffn_phase(0, 0, from_x)
            elif ph == "ffn2":
                ffn_phase(1, 2, from_x)
            elif ph == "mix":
                if from_x:
                    raise ValueError("mixer phase needs the residual in `out` (run an FFN phase first)")
                mixer_phase()
            from_x = False
        final_norm_phase()
        P.barrier()
        P.emit()
    return nc


def _fm(a):
    t, d = a.shape
    return np.ascontiguousarray(a.reshape(t, d // 128, 128).transpose(2, 1, 0))


def _amask(cfg):
    def mult(dl):
        return ((dl >= 0) & (dl <= 128)).astype(np.float32) + ((dl >= 0) & (dl <= 512) & (dl % 4 == 0)) + \
               ((dl >= 0) & (dl <= 2048) & (dl % 16 == 0))
    k = np.arange(128)[:, None]
    qq = np.arange(128)[None, :]
    tiles = [mult(off * 128 + qq - k) for off in range(-3, cfg.NTILE)]
    return np.ascontiguousarray(np.concatenate(tiles, axis=1), dtype=np.float32)


def _cst():
    j = np.arange(128)[:, None]
    i = np.arange(128)[None, :]
    ident = (i == j).astype(np.float32)
    U = (j <= i).astype(np.float32)
    mneg = np.where(i >= j, 0.0, -30000.0).astype(np.float32)
    strict = (i > j).astype(np.float32)
    return np.ascontiguousarray(np.stack([ident, U, mneg, strict], axis=1))


def prep_inputs(cfg, inputs):
    D, KC, B, NHD = cfg.D, cfg.KC, cfg.B, cfg.NHD
    f32 = lambda a: np.ascontiguousarray(np.asarray(a), dtype=np.float32)
    x = f32(inputs["x"])
    nrm = np.ascontiguousarray(np.stack([f32(inputs[k]).reshape(KC, 128).T for k in
                                         ("ffn1_norm", "mix_norm", "ffn2_norm", "final_norm")], axis=1))
    shared = {"norms": nrm}
    for f, pre in ((1, "ffn1"), (2, "ffn2")):
        shared[f"wg{f}"] = f32(inputs[pre + "_w_gate"]); shared[f"wu{f}"] = f32(inputs[pre + "_w_up"])
        shared[f"wd{f}"] = f32(inputs[pre + "_w_down"])
    shared["win"] = f32(inputs["w_in"]); shared["wo"] = f32(inputs["w_out"])
    cw = f32(inputs["conv_w"])
    shared["convw"] = np.ascontiguousarray(cw.reshape(4, 3 * NHD, 128).transpose(2, 1, 0))
    shared["hp"] = np.ascontiguousarray(np.broadcast_to(np.stack([f32(inputs["a_log"]), f32(inputs["dt_bias"])])[None], (128, 2, NHD)))
    shared["dnw"] = np.ascontiguousarray(np.broadcast_to(f32(inputs["dn_norm"])[None, :], (128, 128)))
    shared["cst"] = _cst(); shared["amask"] = _amask(cfg)
    maps = []
    for b in range(B):
        m = dict(shared)
        m["xT"] = _fm(x[b])
        maps.append(m)
    return maps


def assemble(cfg, results):
    out = np.empty((cfg.B, cfg.S, cfg.D), np.float32)
    for b in range(cfg.B):
        o = np.asarray(results[b]["out"]).reshape(128, cfg.KC, cfg.S)
        out[b] = o.transpose(2, 1, 0).reshape(cfg.S, cfg.D)
    return out


def kernel(**inputs):
    cfg = Cfg()
    nc = build(cfg)
    maps = prep_inputs(cfg, inputs)
    res = run_bass_kernel_spmd(nc, maps, core_ids=list(range(cfg.B)))
    return assemble(cfg, res.results)
```
